# Optimizing a Trainium2 kernel written in Bass

```python
import math
import numpy as np
import jax
import jax.numpy as jnp
from jax import lax

D_MODEL = 2048
BATCH = 8
SEQ = 2048
DEPTH = 2

HEAD_DIM = 128
NSA_HEADS = D_MODEL // (2 * HEAD_DIM)
NSA_KV_HEADS = 2
NSA_GROUP = NSA_HEADS // NSA_KV_HEADS
DIFF_VDIM = 2 * HEAD_DIM
DIFF_HEADS = D_MODEL // (2 * DIFF_VDIM)
CMP_LEN = 32
CMP_STRIDE = 16
CMP_HIDDEN = 256
SLC_LEN = 64
SLC_TOPK = 16
WINDOW = 512
D_FF = 5632
Q_BLOCK = 128
ROPE_THETA = 10000.0
EPS = 1e-6
NEG = -1e30

SPLIT_SIZES = (NSA_HEADS * HEAD_DIM,
               NSA_KV_HEADS * HEAD_DIM, NSA_KV_HEADS * HEAD_DIM,
               NSA_KV_HEADS * HEAD_DIM, NSA_KV_HEADS * HEAD_DIM,
               NSA_KV_HEADS * HEAD_DIM, NSA_KV_HEADS * HEAD_DIM,
               3 * NSA_HEADS,
               2 * DIFF_HEADS * HEAD_DIM, 2 * DIFF_HEADS * HEAD_DIM,
               DIFF_HEADS * DIFF_VDIM)
IN_WIDTH = sum(SPLIT_SIZES)

kernel_name = 'hybrid_nsa_diff_macaron'


def _rmsnorm(x, g):
    xf = x.astype(jnp.float32)
    y = xf * lax.rsqrt(jnp.mean(xf * xf, axis=-1, keepdims=True) + EPS)
    return (y * g.astype(jnp.float32)).astype(x.dtype)


def _swiglu(h, w_gate, w_up, w_down):
    return (jax.nn.silu(h @ w_gate) * (h @ w_up)) @ w_down


def _rope_tables(T):
    inv = 1.0 / (ROPE_THETA ** (jnp.arange(0, HEAD_DIM, 2, dtype=jnp.float32) / HEAD_DIM))
    ang = jnp.arange(T, dtype=jnp.float32)[:, None] * inv[None, :]
    ang = jnp.concatenate([ang, ang], axis=-1)
    return jnp.cos(ang), jnp.sin(ang)


def _rope(x, cos, sin):
    x1, x2 = jnp.split(x, 2, axis=-1)
    rot = jnp.concatenate([-x2, x1], axis=-1)
    return (x.astype(jnp.float32) * cos + rot.astype(jnp.float32) * sin).astype(x.dtype)


def _heads(t, n):
    B, T, _ = t.shape
    return t.reshape(B, T, n, -1).transpose(0, 2, 1, 3)


def _masked_softmax(s, mask):
    s = jnp.where(mask, s.astype(jnp.float32), NEG)
    return jax.nn.softmax(s, axis=-1) * mask


def _block_overlap(n_cmp, n_slc):
    c0 = np.arange(n_cmp) * CMP_STRIDE
    s0 = np.arange(n_slc) * SLC_LEN
    lo = np.maximum(c0[:, None], s0[None, :])
    hi = np.minimum(c0[:, None] + CMP_LEN, s0[None, :] + SLC_LEN)
    return (np.clip(hi - lo, 0, None) / CMP_LEN).astype(np.float32)


def _nsa(q, k_cmp, v_cmp, k_slc, v_slc, k_win, v_win, gates,
         pos_k, pos_v, wk1, wk2, wv1, wv2):
    B, G, R, T, dk = q.shape
    scale = dk ** -0.5
    n_cmp = (T - CMP_LEN) // CMP_STRIDE + 1
    n_slc = T // SLC_LEN
    n_sel = min(SLC_TOPK, n_slc)
    n_qb = T // Q_BLOCK
    t_pos = jnp.arange(T)

    win_idx = jnp.arange(n_cmp)[:, None] * CMP_STRIDE + jnp.arange(CMP_LEN)[None, :]

    def compress(kv, pos, w1, w2):
        blocks = kv[:, :, win_idx, :] + pos
        flat = blocks.reshape(B, G, n_cmp, CMP_LEN * dk)
        return jax.nn.silu(flat @ w1) @ w2

    kc = compress(k_cmp, pos_k, wk1, wk2)
    vc = compress(v_cmp, pos_v, wv1, wv2)
    s_c = jnp.einsum('bgrtd,bgnd->bgrtn', q, kc) * scale
    c_end = jnp.arange(n_cmp) * CMP_STRIDE + CMP_LEN - 1
    c_mask = c_end[None, :] <= t_pos[:, None]
    p_c = _masked_softmax(s_c, c_mask)
    o_cmp = jnp.einsum('bgrtn,bgnd->bgrtd', p_c.astype(vc.dtype), vc)

    overlap = jnp.asarray(_block_overlap(n_cmp, n_slc), dtype=jnp.float32)
    imp = jnp.einsum('bgrtn,ns->bgts', p_c, overlap)
    blk = jnp.arange(n_slc)
    cur = t_pos // SLC_LEN
    forced = (blk[None, :] == 0) | (blk[None, :] == cur[:, None]) | (blk[None, :] == cur[:, None] - 1)
    causal_blk = blk[None, :] * SLC_LEN <= t_pos[:, None]
    imp = jnp.where(forced, 1e4, imp)
    imp = jnp.where(causal_blk, imp, -1.0)
    top_val, top_idx = lax.top_k(imp, n_sel)
    top_ok = top_val >= 0.0

    kb = k_slc.reshape(B, G, n_slc, SLC_LEN, dk)
    vb = v_slc.reshape(B, G, n_slc, SLC_LEN, dk)

    def sel_block(i):
        b = i // n_qb
        s0 = (i % n_qb) * Q_BLOCK
        qi = lax.dynamic_slice(q, (b, 0, 0, s0, 0), (1, G, R, Q_BLOCK, dk))[0]
        ii = lax.dynamic_slice(top_idx, (b, 0, s0, 0), (1, G, Q_BLOCK, n_sel))[0]
        ok = lax.dynamic_slice(top_ok, (b, 0, s0, 0), (1, G, Q_BLOCK, n_sel))[0]
        kbb = lax.dynamic_index_in_dim(kb, b, 0, keepdims=False)
        vbb = lax.dynamic_index_in_dim(vb, b, 0, keepdims=False)
        ks = jax.vmap(lambda kg, ig: kg[ig])(kbb, ii)
        vs = jax.vmap(lambda vg, ig: vg[ig])(vbb, ii)
        tq = s0 + jnp.arange(Q_BLOCK)
        kpos = ii[..., None] * SLC_LEN + jnp.arange(SLC_LEN)
        mask = ok[..., None] & (kpos <= tq[None, :, None, None])
        mask = mask.reshape(G, 1, Q_BLOCK, n_sel * SLC_LEN)
        s = jnp.einsum('grqd,gqnld->grqnl', qi, ks) * scale
        p = _masked_softmax(s.reshape(G, R, Q_BLOCK, n_sel * SLC_LEN), mask)
        p = p.reshape(G, R, Q_BLOCK, n_sel, SLC_LEN)
        return jnp.einsum('grqnl,gqnld->grqd', p.astype(vs.dtype), vs)

    o = lax.map(sel_block, jnp.arange(B * n_qb))
    o_slc = o.reshape(B, n_qb, G, R, Q_BLOCK, dk).transpose(0, 2, 3, 1, 4, 5).reshape(B, G, R, T, dk)

    kp = jnp.pad(k_win, ((0, 0), (0, 0), (WINDOW, 0), (0, 0)))
    vp = jnp.pad(v_win, ((0, 0), (0, 0), (WINDOW, 0), (0, 0)))
    span = WINDOW + Q_BLOCK

    def win_block(j):
        s0 = j * Q_BLOCK
        qi = lax.dynamic_slice_in_dim(q, s0, Q_BLOCK, axis=3)
        ki = lax.dynamic_slice_in_dim(kp, s0, span, axis=2)
        vi = lax.dynamic_slice_in_dim(vp, s0, span, axis=2)
        tq = s0 + jnp.arange(Q_BLOCK)
        tk = s0 - WINDOW + jnp.arange(span)
        dist = tq[:, None] - tk[None, :]
        mask = (tk[None, :] >= 0) & (dist >= 0) & (dist < WINDOW)
        s = jnp.einsum('bgrqd,bgkd->bgrqk', qi, ki) * scale
        p = _masked_softmax(s, mask)
        return jnp.einsum('bgrqk,bgkd->bgrqd', p.astype(vi.dtype), vi)

    o = lax.map(win_block, jnp.arange(n_qb))
    o_win = o.transpose(1, 2, 3, 0, 4, 5).reshape(B, G, R, T, dk)

    g = gates.reshape(B, T, G, R, 3).transpose(0, 2, 3, 1, 4)
    out = g[..., 0:1] * o_cmp + g[..., 1:2] * o_slc + g[..., 2:3] * o_win
    return out.astype(q.dtype).transpose(0, 3, 1, 2, 4).reshape(B, T, G * R * dk)


def _diff_attn(q, k, v, lam_q1, lam_k1, lam_q2, lam_k2, subln_g, lambda_init):
    B, Hd, _, T, dk = q.shape
    scale = dk ** -0.5
    n_qb = T // Q_BLOCK
    f32 = jnp.float32
    lam = (jnp.exp(jnp.sum(lam_q1.astype(f32) * lam_k1.astype(f32)))
           - jnp.exp(jnp.sum(lam_q2.astype(f32) * lam_k2.astype(f32))) + lambda_init)
    t_k = jnp.arange(T)

    def blk(j):
        s0 = j * Q_BLOCK
        qi = lax.dynamic_slice_in_dim(q, s0, Q_BLOCK, axis=3)
        s = jnp.einsum('bhcqd,bhckd->bhcqk', qi, k) * scale
        mask = t_k[None, :] <= (s0 + jnp.arange(Q_BLOCK))[:, None]
        p = _masked_softmax(s, mask)
        a = p[:, :, 0] - lam * p[:, :, 1]
        return jnp.einsum('bhqk,bhkd->bhqd', a.astype(v.dtype), v)

    o = lax.map(blk, jnp.arange(n_qb))
    o = o.transpose(1, 2, 0, 3, 4).reshape(B, Hd, T, 2 * dk)
    o = (_rmsnorm(o, subln_g) * (1.0 - lambda_init)).astype(v.dtype)
    return o.transpose(0, 2, 1, 3).reshape(B, T, Hd * 2 * dk)


def _token_mix(h, w_in, pos_k, pos_v, wk1, wk2, wv1, wv2,
               lam_q1, lam_k1, lam_q2, lam_k2, subln_g, w_out, lambda_init, cos, sin):
    B, T, _ = h.shape
    proj = h @ w_in
    cuts = [int(c) for c in np.cumsum(SPLIT_SIZES)[:-1]]
    (q_n, kc, vc, ks, vs, kw, vw, g, q_d, k_d, v_d) = jnp.split(proj, cuts, axis=-1)
    G, R = NSA_KV_HEADS, NSA_GROUP
    qn = _rope(_heads(q_n, NSA_HEADS), cos, sin).reshape(B, G, R, T, HEAD_DIM)
    kc = _rope(_heads(kc, G), cos, sin)
    ks = _rope(_heads(ks, G), cos, sin)
    kw = _rope(_heads(kw, G), cos, sin)
    gates = jax.nn.sigmoid(g.astype(jnp.float32)).reshape(B, T, NSA_HEADS, 3)
    o_nsa = _nsa(qn, kc, _heads(vc, G), ks, _heads(vs, G), kw, _heads(vw, G), gates,
                 pos_k, pos_v, wk1, wk2, wv1, wv2)
    qd = _rope(_heads(q_d, 2 * DIFF_HEADS), cos, sin).reshape(B, DIFF_HEADS, 2, T, HEAD_DIM)
    kd = _rope(_heads(k_d, 2 * DIFF_HEADS), cos, sin).reshape(B, DIFF_HEADS, 2, T, HEAD_DIM)
    vd = _heads(v_d, DIFF_HEADS)
    o_diff = _diff_attn(qd, kd, vd, lam_q1, lam_k1, lam_q2, lam_k2, subln_g, lambda_init)
    return jnp.concatenate([o_nsa, o_diff], axis=-1) @ w_out


def setup_inputs(seed: int = 0) -> dict:
    key = jax.random.key(seed)
    ks = jax.random.split(key, 24)
    f32 = jnp.float32

    def nrm(k, shape, scale):
        return jax.random.normal(k, shape, f32) * scale

    def gain(k, shape):
        return 1.0 + 0.05 * jax.random.normal(k, shape, f32)

    L, D = DEPTH, D_MODEL
    cin = CMP_LEN * HEAD_DIM
    return {
        'x': nrm(ks[0], (BATCH, SEQ, D), 1.0),
        'ffn1_norm': gain(ks[1], (L, D)),
        'ffn1_w_gate': nrm(ks[2], (L, D, D_FF), D ** -0.5),
        'ffn1_w_up': nrm(ks[3], (L, D, D_FF), D ** -0.5),
        'ffn1_w_down': nrm(ks[4], (L, D_FF, D), D_FF ** -0.5),
        'mix_norm': gain(ks[5], (L, D)),
        'w_in': nrm(ks[6], (L, D, IN_WIDTH), D ** -0.5),
        'cmp_pos_k': nrm(ks[7], (L, CMP_LEN, HEAD_DIM), 0.1),
        'cmp_pos_v': nrm(ks[8], (L, CMP_LEN, HEAD_DIM), 0.1),
        'cmp_wk1': nrm(ks[9], (L, cin, CMP_HIDDEN), cin ** -0.5),
        'cmp_wk2': nrm(ks[10], (L, CMP_HIDDEN, HEAD_DIM), CMP_HIDDEN ** -0.5),
        'cmp_wv1': nrm(ks[11], (L, cin, CMP_HIDDEN), cin ** -0.5),
        'cmp_wv2': nrm(ks[12], (L, CMP_HIDDEN, HEAD_DIM), CMP_HIDDEN ** -0.5),
        'lam_q1': nrm(ks[13], (L, HEAD_DIM), 0.1),
        'lam_k1': nrm(ks[14], (L, HEAD_DIM), 0.1),
        'lam_q2': nrm(ks[15], (L, HEAD_DIM), 0.1),
        'lam_k2': nrm(ks[16], (L, HEAD_DIM), 0.1),
        'diff_subln': gain(ks[17], (L, DIFF_VDIM)),
        'w_out': nrm(ks[18], (L, D, D), D ** -0.5),
        'ffn2_norm': gain(ks[19], (L, D)),
        'ffn2_w_gate': nrm(ks[20], (L, D, D_FF), D ** -0.5),
        'ffn2_w_up': nrm(ks[21], (L, D, D_FF), D ** -0.5),
        'ffn2_w_down': nrm(ks[22], (L, D_FF, D), D_FF ** -0.5),
        'final_norm': gain(ks[23], (D,)),
    }


def reference(x, ffn1_norm, ffn1_w_gate, ffn1_w_up, ffn1_w_down, mix_norm, w_in,
              cmp_pos_k, cmp_pos_v, cmp_wk1, cmp_wk2, cmp_wv1, cmp_wv2,
              lam_q1, lam_k1, lam_q2, lam_k2, diff_subln, w_out,
              ffn2_norm, ffn2_w_gate, ffn2_w_up, ffn2_w_down, final_norm):
    T = x.shape[1]
    cos, sin = _rope_tables(T)
    for l in range(DEPTH):
        lambda_init = 0.8 - 0.6 * math.exp(-0.3 * l)
        x = x + 0.5 * _swiglu(_rmsnorm(x, ffn1_norm[l]), ffn1_w_gate[l], ffn1_w_up[l], ffn1_w_down[l])
        h = _rmsnorm(x, mix_norm[l])
        x = x + _token_mix(h, w_in[l], cmp_pos_k[l], cmp_pos_v[l], cmp_wk1[l], cmp_wk2[l],
                           cmp_wv1[l], cmp_wv2[l], lam_q1[l], lam_k1[l], lam_q2[l], lam_k2[l],
                           diff_subln[l], w_out[l], lambda_init, cos, sin)
        x = x + 0.5 * _swiglu(_rmsnorm(x, ffn2_norm[l]), ffn2_w_gate[l], ffn2_w_up[l], ffn2_w_down[l])
    return _rmsnorm(x, final_norm)
```

```python
import math
from contextlib import ExitStack
import numpy as np
import concourse.bass as bass
import concourse.mybir as mybir
from concourse.bass_utils import run_bass_kernel_spmd

F32 = mybir.dt.float32
BF16 = mybir.dt.bfloat16
AF = mybir.ActivationFunctionType
ALU = mybir.AluOpType
AX = mybir.AxisListType

ENGS = ("pe", "act", "dve", "pool", "sp")
EPOCH = 12000
STRICT_SAME_ENGINE = True

D = 2048
NDC = 16
T = 2048
NTT = 16
DFF = 5632
HD = 128
EPS = 1e-6
NEGB = -30000.0
SCALE = HD ** -0.5
IN_WIDTH = 5656
C_QN, C_KC, C_VC, C_KS, C_VS, C_KW, C_VW, C_G, C_QD, C_KD, C_VD = (
    0, 1024, 1280, 1536, 1792, 2048, 2304, 2560, 2584, 3608, 4632)


class Buf:
    __slots__ = ("w", "rs", "excl")

    def __init__(self, excl=False):
        self.w = None
        self.rs = []
        self.excl = excl


class DSem:
    __slots__ = ("sem", "count")

    def __init__(self, sem):
        self.sem = sem
        self.count = 0


class Op:
    __slots__ = ("eng", "fn", "deps", "needs_inc", "tok", "dsem")


class SemPool:
    def __init__(self, nc, es):
        self.nc, self.es, self.n = nc, es, 0

    def pop(self):
        self.n += 1
        return self.es.enter_context(self.nc.semaphore(f"sem{self.n}"))


class Prog:
    def __init__(self, nc, sem_pool):
        self.nc = nc
        self.sem_pool = sem_pool
        self.ops = {e: [] for e in ENGS}
        self.ctr_sems = {e: [] for e in ENGS}
        self.ctr = {e: 0 for e in ENGS}
        self.waited = {e: {} for e in ENGS}
        self.dsems = []
        self.free_dsems = []

    def new_dsem(self):
        if self.free_dsems:
            d = self.free_dsems.pop()
        else:
            d = DSem(self.sem_pool.pop())
        self.dsems.append(d)
        return d

    def op(self, eng, fn, reads=(), writes=(), dsem=None, ndma=1):
        o = Op()
        o.eng = eng
        o.fn = fn
        o.needs_inc = False
        o.dsem = dsem
        deps = {}
        for b in reads:
            if b.w is not None:
                deps[id(b.w)] = b.w
            if b.excl:
                for r in b.rs:
                    if r.eng != eng:
                        deps[id(r)] = r
        for b in writes:
            if b.w is not None:
                deps[id(b.w)] = b.w
            for r in b.rs:
                deps[id(r)] = r
        dl = []
        for d in deps.values():
            if d.eng == eng and d.dsem is None:
                if eng == "pe" or not STRICT_SAME_ENGINE:
                    continue
            dl.append(d)
            d.needs_inc = True
        o.deps = dl
        for b in reads:
            if dsem is None:
                b.rs = [r for r in b.rs if not (r.eng == eng and r.dsem is None)]
            b.rs.append(o)
        for b in writes:
            b.w = o
            b.rs = []
        if dsem is not None:
            dsem.count += 16 * ndma
            o.tok = (dsem.sem, dsem.count)
        else:
            o.tok = None
        self.ops[eng].append(o)
        return o

    def _assign_tokens(self):
        for e in ENGS:
            for o in self.ops[e]:
                if o.dsem is None and o.needs_inc:
                    c = self.ctr[e]
                    ep = c // EPOCH
                    while len(self.ctr_sems[e]) <= ep:
                        self.ctr_sems[e].append(self.sem_pool.pop())
                    o.tok = (self.ctr_sems[e][ep], c % EPOCH + 1)
                    self.ctr[e] = c + 1

    def _emit_engine(self, e, engobj):
        waited = self.waited[e]
        for o in self.ops[e]:
            for d in o.deps:
                sem, val = d.tok
                k = id(sem)
                if waited.get(k, 0) < val:
                    engobj.wait_ge(sem, val)
                    waited[k] = val
            r = o.fn(engobj)
            if o.dsem is not None:
                if not isinstance(r, (list, tuple)):
                    r = [r]
                for ins in r:
                    ins.then_inc(o.dsem.sem, 16)
            elif o.needs_inc:
                r.then_inc(o.tok[0], 1)

    def flush(self):
        self._assign_tokens()
        nc = self.nc
        finals = [(d.sem, d.count) for d in self.dsems if d.count > 0]
        with nc.Block() as block:
            @block.tensor
            def _(eng):
                self._emit_engine("pe", eng)

            @block.scalar
            def _(eng):
                self._emit_engine("act", eng)

            @block.vector
            def _(eng):
                self._emit_engine("dve", eng)

            @block.gpsimd
            def _(eng):
                self._emit_engine("pool", eng)

            @block.sync
            def _(eng):
                self._emit_engine("sp", eng)
                for (sem, val) in finals:
                    eng.wait_ge(sem, val)
        self.ops = {e: [] for e in ENGS}
        self.free_dsems.extend(self.dsems)
        self.dsems = []


class Ctx:
    def __init__(self, nc, P, nbf=0, nbanks=None):
        self.nc = nc
        self.P = P
        self.es = ExitStack()
        self.n = 0
        self.bank_i = 0
        self.nbf = nbf
        self.nbanks = (8 - nbf) if nbanks is None else nbanks

    def __enter__(self):
        self.es.__enter__()
        Ctx.uid = getattr(Ctx, "uid", 0) + 1
        self.u = Ctx.uid
        self.banks = [self.es.enter_context(self.nc.psum_tensor(f"bk{i}_{self.u}", [128, 512], F32))
                      for i in range(self.nbanks)]
        self.bfbanks = [self.es.enter_context(self.nc.psum_tensor(f"bfk{i}_{self.u}", [128, 1024], BF16))
                        for i in range(self.nbf)]
        self.b_bank = [Buf(excl=True) for _ in range(8)]
        self.b_bfbank = [Buf(excl=True) for _ in range(self.nbf)]
        return self

    def __exit__(self, *a):
        return self.es.__exit__(*a)

    def sb(self, shape, dt):
        self.n += 1
        return self.es.enter_context(self.nc.sbuf_tensor(f"t{self.n}_{self.u}", shape, dt))

    def next_bank(self):
        i = self.bank_i
        self.bank_i = (i + 1) % self.nbanks
        return i

    def ps(self, shape, dt=F32):
        self.n += 1
        return self.es.enter_context(self.nc.psum_tensor(f"p{self.n}_{self.u}", shape, dt))


def emit_norm(P, C, xT_v, rd_bufs, g_pc, t_lo, t_hi, out_fn, NG=256, post_fn=None):
    st = getattr(C, "_norm", None)
    if st is None:
        st = {}
        C._norm = st
        st["ones"] = C.sb([128, 128], BF16)
        st["epsb"] = C.sb([128, 1], F32)
        st["gsb"] = C.sb([128, NDC], F32)
        st["xin"] = C.sb([128, NDC, NG], F32)
        st["sq"] = [C.sb([128, NG], BF16) for _ in range(2)]
        st["rstd"] = C.sb([128, NG], F32)
        st["b_c"], st["b_g"], st["b_xin"], st["b_rstd"] = Buf(), Buf(), Buf(), Buf()
        st["b_sq"] = [Buf(), Buf()]
        st["d_g"] = P.new_dsem()
        st["d_xin"] = P.new_dsem()
        st["sqc"] = 0
        P.op("pool", lambda e: e.memset(st["epsb"][:], EPS), writes=[st["b_c"]])
        P.op("pool", lambda e: e.memset(st["ones"][:], 1.0), writes=[st["b_c"]])
    ones, epsb, gsb, xin, sq, rstd = st["ones"], st["epsb"], st["gsb"], st["xin"], st["sq"], st["rstd"]
    b_c, b_g, b_xin, b_rstd, b_sq = st["b_c"], st["b_g"], st["b_xin"], st["b_rstd"], st["b_sq"]
    P.op("sp", lambda e: e.dma_start(out=gsb[:], in_=g_pc), writes=[b_g], dsem=st["d_g"])
    for ta in range(t_lo, t_hi, NG):
        P.op("sp", lambda e, ta=ta: e.dma_start(out=xin[:], in_=xT_v[:, :, ta:ta + NG]),
             reads=rd_bufs(ta), writes=[b_xin], dsem=st["d_xin"])
        bi = C.next_bank()
        for dc in range(NDC):
            s = st["sqc"] % 2
            st["sqc"] += 1
            P.op("act", lambda e, s=s, dc=dc: e.activation(out=sq[s][:], in_=xin[:, dc, :], func=AF.Square),
                 reads=[b_xin], writes=[b_sq[s]])
            P.op("pe", lambda e, s=s, dc=dc, bi=bi: e.matmul(C.banks[bi][:, 0:NG], lhsT=ones[:], rhs=sq[s][:],
                                                            start=(dc == 0), stop=(dc == NDC - 1)),
                 reads=[b_c, b_sq[s]], writes=[C.b_bank[bi]])
        P.op("act", lambda e, bi=bi: e.activation(out=rstd[:], in_=C.banks[bi][:, 0:NG], func=AF.Sqrt,
                                                  scale=1.0 / D, bias=epsb[:]),
             reads=[C.b_bank[bi], b_c], writes=[b_rstd])
        P.op("dve", lambda e: e.reciprocal(out=rstd[:], in_=rstd[:]), reads=[b_rstd], writes=[b_rstd])
        for dc in range(NDC):
            o_ap, o_bufs = out_fn(dc, ta, NG)
            P.op("dve", lambda e, dc=dc, o_ap=o_ap: e.scalar_tensor_tensor(
                out=o_ap, in0=xin[:, dc, :], scalar=gsb[:, dc:dc + 1], in1=rstd[:], op0=ALU.mult, op1=ALU.mult),
                reads=[b_xin, b_g, b_rstd], writes=o_bufs)
            if post_fn is not None:
                post_fn(dc, ta, NG)


def emit_ffn(P, nc, xT, xT_bufs, wg, wu, wd, g_pc, F=DFF, TT=1024, coef=0.5, Tn=T):
    NFC = F // 128
    NTG = TT // 512
    xT_v = xT.rearrange("(c p) t -> p c t", p=128)
    wg_v = wg.rearrange("(c p) f -> p c f", p=128)
    wu_v = wu.rearrange("(c p) f -> p c f", p=128)
    with Ctx(nc, P) as C:
        xnT = C.sb([128, NDC, TT], BF16)
        hT = C.sb([128, NFC, TT], BF16)
        NWS = 2
        wgs = [C.sb([128, NDC, 128], BF16) for _ in range(NWS)]
        wus = [C.sb([128, NDC, 128], BF16) for _ in range(NWS)]
        sg = [C.sb([128, 512], F32) for _ in range(2)]
        b_xnT = [[Buf() for _ in range(NTG)] for _ in range(NDC)]
        b_hT = [[Buf() for _ in range(NTG)] for _ in range(NFC)]
        b_wg = [Buf() for _ in wgs]
        b_wu = [Buf() for _ in wus]
        b_sg = [Buf() for _ in sg]
        d_wg = [P.new_dsem() for _ in wgs]
        d_wu = [P.new_dsem() for _ in wus]
        wcount = 0
        for h in range(Tn // TT):
            t0 = h * TT
            emit_norm(P, C, xT_v, lambda ta: [xT_bufs[dc][ta // 512] for dc in range(NDC)], g_pc, t0, t0 + TT,
                      lambda dc, ta, NG, t0=t0: (xnT[:, dc, ta - t0:ta - t0 + NG], [b_xnT[dc][(ta - t0) // 512]]))
            for fc in range(NFC):
                s = wcount % NWS
                wcount += 1
                P.op("pool", lambda e, s=s, fc=fc: e.dma_start(out=wgs[s][:], in_=wg_v[:, :, fc * 128:(fc + 1) * 128]),
                     writes=[b_wg[s]], dsem=d_wg[s])
                P.op("pool", lambda e, s=s, fc=fc: e.dma_start(out=wus[s][:], in_=wu_v[:, :, fc * 128:(fc + 1) * 128]),
                     writes=[b_wu[s]], dsem=d_wu[s])
                for tg in range(NTG):
                    bg = C.next_bank()
                    bu = C.next_bank()
                    for dc in range(NDC):
                        P.op("pe", lambda e, s=s, dc=dc, tg=tg, bg=bg: e.matmul(
                            C.banks[bg][:], lhsT=wgs[s][:, dc, :], rhs=xnT[:, dc, tg * 512:(tg + 1) * 512],
                            start=(dc == 0), stop=(dc == NDC - 1)),
                            reads=[b_wg[s], b_xnT[dc][tg]], writes=[C.b_bank[bg]])
                    for dc in range(NDC):
                        P.op("pe", lambda e, s=s, dc=dc, tg=tg, bu=bu: e.matmul(
                            C.banks[bu][:], lhsT=wus[s][:, dc, :], rhs=xnT[:, dc, tg * 512:(tg + 1) * 512],
                            start=(dc == 0), stop=(dc == NDC - 1)),
                            reads=[b_wu[s], b_xnT[dc][tg]], writes=[C.b_bank[bu]])
                    q = (fc * NTG + tg) % 2
                    P.op("act", lambda e, q=q, bg=bg: e.activation(out=sg[q][:], in_=C.banks[bg][:], func=AF.Silu),
                         reads=[C.b_bank[bg]], writes=[b_sg[q]])
                    P.op("dve", lambda e, q=q, bu=bu, fc=fc, tg=tg: e.tensor_tensor(
                        out=hT[:, fc, tg * 512:(tg + 1) * 512], in0=sg[q][:], in1=C.banks[bu][:], op=ALU.mult),
                        reads=[b_sg[q], C.b_bank[bu]], writes=[b_hT[fc][tg]])
            emit_down_cached(P, C, xT, xT_bufs, wd, NFC,
                             lambda c, tg: (hT[:, c, tg * 512:(tg + 1) * 512], b_hT[c][tg]),
                             [t0 + i * 512 for i in range(NTG)], coef)
        P.flush()


def emit_down_cached(P, C, xT, xT_bufs, w, NC_, rhs_fn, toks, coef):
    st = getattr(C, "_down", None)
    if st is None:
        st = {}
        C._down = st
        NWS = 2
        st["wds"] = [C.sb([128, NC_, 128], BF16) for _ in range(NWS)]
        st["xres"] = [C.sb([128, 512], F32) for _ in range(2)]
        st["xo"] = [C.sb([128, 512], F32) for _ in range(2)]
        st["b_wd"] = [Buf() for _ in range(NWS)]
        st["b_xres"] = [Buf(), Buf()]
        st["b_xo"] = [Buf(), Buf()]
        st["d_wd"] = [P.new_dsem() for _ in range(NWS)]
        st["d_xres"] = [P.new_dsem() for _ in range(2)]
        st["d_xo"] = [P.new_dsem() for _ in range(2)]
        st["cnt"] = 0
        st["wc"] = 0
    w_v = w.rearrange("(c p) n -> p c n", p=128)
    wds, xres, xo = st["wds"], st["xres"], st["xo"]
    b_wd, b_xres, b_xo = st["b_wd"], st["b_xres"], st["b_xo"]
    d_wd, d_xres, d_xo = st["d_wd"], st["d_xres"], st["d_xo"]
    for dc in range(NDC):
        s = st["wc"] % 2
        st["wc"] += 1
        P.op("pool", lambda e, s=s, dc=dc: e.dma_start(out=wds[s][:], in_=w_v[:, :, dc * 128:(dc + 1) * 128]),
             writes=[b_wd[s]], dsem=d_wd[s])
        for tg, ta in enumerate(toks):
            by = C.next_bank()
            for c in range(NC_):
                r_ap, r_buf = rhs_fn(c, tg)
                P.op("pe", lambda e, s=s, c=c, by=by, r_ap=r_ap: e.matmul(
                    C.banks[by][:], lhsT=wds[s][:, c, :], rhs=r_ap, start=(c == 0), stop=(c == NC_ - 1)),
                    reads=[b_wd[s], r_buf], writes=[C.b_bank[by]])
            q = st["cnt"] % 2
            st["cnt"] += 1
            xb = xT_bufs[dc][ta // 512]
            P.op("sp", lambda e, q=q, dc=dc, ta=ta: e.dma_start(out=xres[q][:], in_=xT[dc * 128:(dc + 1) * 128, ta:ta + 512]),
                 reads=[xb], writes=[b_xres[q]], dsem=d_xres[q])
            P.op("dve", lambda e, q=q, by=by: e.scalar_tensor_tensor(
                out=xo[q][:], in0=C.banks[by][:], scalar=coef, in1=xres[q][:], op0=ALU.mult, op1=ALU.add),
                reads=[C.b_bank[by], b_xres[q]], writes=[b_xo[q]])
            P.op("sp", lambda e, q=q, dc=dc, ta=ta: e.dma_start(out=xT[dc * 128:(dc + 1) * 128, ta:ta + 512], in_=xo[q][:]),
                 reads=[b_xo[q]], writes=[xb], dsem=d_xo[q])


def emit_proj(P, nc, xT, xT_bufs, w_in, g_pc, cosT, sinT, S, Kc):
    xT_v = xT.rearrange("(c p) t -> p c t", p=128)
    w_v = w_in.rearrange("(c p) n -> p c n", p=128)
    with Ctx(nc, P) as C:
        hnT = C.sb([128, NDC, T], BF16)
        cos_sb = C.sb([128, T], F32)
        sin_sb = C.sb([128, T], F32)
        b_hn = [[Buf() for _ in range(T // 512)] for _ in range(NDC)]
        b_cs = Buf()
        d_cs = P.new_dsem()
        P.op("sp", lambda e: [e.dma_start(out=cos_sb[:], in_=cosT), e.dma_start(out=sin_sb[:], in_=sinT)],
             writes=[b_cs], dsem=d_cs, ndma=2)
        emit_norm(P, C, xT_v, lambda ta: [xT_bufs[dc][ta // 512] for dc in range(NDC)], g_pc, 0, T,
                  lambda dc, ta, NG: (hnT[:, dc, ta:ta + NG], [b_hn[dc][ta // 512]]))
        heads = []
        for h in range(8):
            heads.append((h, C_QN + h * 128, True))
        for g in range(2):
            heads.append((8 + g, C_KC + g * 128, True))
            heads.append((10 + g, C_KS + g * 128, True))
            heads.append((12 + g, C_KW + g * 128, True))
            heads.append((30 + g, C_VC + g * 128, False))
        for h in range(8):
            heads.append((14 + h, C_QD + h * 128, True))
            heads.append((22 + h, C_KD + h * 128, True))
        NWS = 3
        wA = [C.sb([128, NDC, 128], BF16) for _ in range(NWS)]
        perm = C.sb([128, 128], BF16)
        abf = [C.sb([128, 512], BF16) for _ in range(2)]
        t1 = [C.sb([128, 512], F32) for _ in range(2)]
        t2 = [C.sb([128, 512], F32) for _ in range(2)]
        stage = [C.sb([128, T], BF16) for _ in range(2)]
        b_wA = [Buf() for _ in range(NWS)]
        b_perm = Buf()
        b_abf = [Buf(), Buf()]
        b_t1 = [Buf(), Buf()]
        b_t2 = [Buf(), Buf()]
        b_st = [Buf(), Buf()]
        d_wA = [P.new_dsem() for _ in range(NWS)]
        d_st = [P.new_dsem() for _ in range(2)]
        d_perm = P.new_dsem()
        P.op("sp", lambda e: e.dma_start(out=perm[:], in_=Kc["permsw"]), writes=[b_perm], dsem=d_perm)
        cnt = 0
        pending = None

        def rot_step(k, ba, bb, q, tg):
            P.op("pe", lambda e, k=k, bb=bb: e.matmul(C.banks[bb][:], lhsT=perm[:], rhs=abf[k][:], start=True, stop=True),
                 reads=[b_perm, b_abf[k]], writes=[C.b_bank[bb]])
            P.op("dve", lambda e, k=k, ba=ba, tg=tg: e.tensor_tensor(
                out=t1[k][:], in0=C.banks[ba][:], in1=cos_sb[:, tg * 512:(tg + 1) * 512], op=ALU.mult),
                reads=[C.b_bank[ba], b_cs], writes=[b_t1[k]])
            P.op("dve", lambda e, k=k, bb=bb, tg=tg: e.tensor_tensor(
                out=t2[k][:], in0=C.banks[bb][:], in1=sin_sb[:, tg * 512:(tg + 1) * 512], op=ALU.mult),
                reads=[C.b_bank[bb], b_cs], writes=[b_t2[k]])
            P.op("pool", lambda e, k=k, q=q, tg=tg: e.tensor_tensor(
                out=stage[q][:, tg * 512:(tg + 1) * 512], in0=t1[k][:], in1=t2[k][:], op=ALU.add),
                reads=[b_t1[k], b_t2[k]], writes=[b_st[q]])

        def issue_w(hj):
            if hj < len(heads):
                sj, cj = hj % NWS, heads[hj][1]
                P.op("pool", lambda e, sj=sj, cj=cj: e.dma_start(out=wA[sj][:], in_=w_v[:, :, cj:cj + 128]),
                     writes=[b_wA[sj]], dsem=d_wA[sj])
        issue_w(0)
        issue_w(1)
        for hi, (fi, c0, rope) in enumerate(heads):
            s = hi % NWS
            q = hi % 2
            issue_w(hi + 2)
            for tg in range(T // 512):
                ba = C.next_bank()
                for dc in range(NDC):
                    P.op("pe", lambda e, s=s, dc=dc, tg=tg, ba=ba: e.matmul(
                        C.banks[ba][:], lhsT=wA[s][:, dc, :], rhs=hnT[:, dc, tg * 512:(tg + 1) * 512],
                        start=(dc == 0), stop=(dc == NDC - 1)),
                        reads=[b_wA[s], b_hn[dc][tg]], writes=[C.b_bank[ba]])
                if rope:
                    k = cnt % 2
                    cnt += 1
                    bb = C.next_bank()
                    P.op("act", lambda e, k=k, ba=ba: e.activation(out=abf[k][:], in_=C.banks[ba][:], func=AF.Copy),
                         reads=[C.b_bank[ba]], writes=[b_abf[k]])
                    if pending is not None:
                        pending()
                    pending = (lambda k=k, ba=ba, bb=bb, q=q, tg=tg, fi=fi: rot_step(k, ba, bb, q, tg))
                    if tg == T // 512 - 1:
                        pending()
                        pending = None
                else:
                    P.op("act", lambda e, q=q, ba=ba, tg=tg: e.activation(
                        out=stage[q][:, tg * 512:(tg + 1) * 512], in_=C.banks[ba][:], func=AF.Copy),
                        reads=[C.b_bank[ba]], writes=[b_st[q]])
            P.op("sp", lambda e, q=q, fi=fi: e.dma_start(out=S["fm"][fi], in_=stage[q][:]),
                 reads=[b_st[q]], writes=[S["b_fm"][fi]], dsem=d_st[q])
        wV = [C.sb([128, NDC, 512], BF16) for _ in range(2)]
        vst = [C.sb([128, 512], BF16) for _ in range(2)]
        gst = [C.sb([128, 24], F32) for _ in range(2)]
        b_wV = [Buf(), Buf()]
        b_vst = [Buf(), Buf()]
        b_gst = [Buf(), Buf()]
        d_wV = [P.new_dsem(), P.new_dsem()]
        d_vst = [P.new_dsem(), P.new_dsem()]
        d_gst = [P.new_dsem(), P.new_dsem()]
        groups = [
            ([(C_VS, 256), (C_VW, 256)], "vsw", 0),
            ([(C_VD, 512)], "vd", 0),
            ([(C_VD + 512, 512)], "vd", 512),
            ([(C_G, 24)], "gates", 0),
        ]
        cnt = 0
        for gi, (cols, dst, dcol) in enumerate(groups):
            s = gi % 2
            ncols = sum(n for _, n in cols)

            def ld(e, s=s, cols=cols):
                r = []
                o = 0
                for (c0, n) in cols:
                    r.append(e.dma_start(out=wV[s][:, :, o:o + n], in_=w_v[:, :, c0:c0 + n]))
                    o += n
                return r
            P.op("pool", ld, writes=[b_wV[s]], dsem=d_wV[s], ndma=len(cols))
            for tt in range(NTT):
                bk = C.next_bank()
                for dc in range(NDC):
                    P.op("pe", lambda e, s=s, dc=dc, tt=tt, bk=bk, ncols=ncols: e.matmul(
                        C.banks[bk][:, 0:ncols], lhsT=hnT[:, dc, tt * 128:(tt + 1) * 128], rhs=wV[s][:, dc, 0:ncols],
                        start=(dc == 0), stop=(dc == NDC - 1)),
                        reads=[b_wV[s], b_hn[dc][tt // 4]], writes=[C.b_bank[bk]])
                q = cnt % 2
                cnt += 1
                if dst == "gates":
                    P.op("act", lambda e, q=q, bk=bk: e.activation(out=gst[q][:], in_=C.banks[bk][:, 0:24], func=AF.Sigmoid),
                         reads=[C.b_bank[bk]], writes=[b_gst[q]])
                    P.op("sp", lambda e, q=q, tt=tt: e.dma_start(out=S["gates"][tt * 128:(tt + 1) * 128, :], in_=gst[q][:]),
                         reads=[b_gst[q]], writes=[S["b_gates"]], dsem=d_gst[q])
                else:
                    P.op("act", lambda e, q=q, bk=bk: e.activation(out=vst[q][:], in_=C.banks[bk][:], func=AF.Copy),
                         reads=[C.b_bank[bk]], writes=[b_vst[q]])
                    P.op("sp", lambda e, q=q, tt=tt, dst=dst, dcol=dcol: e.dma_start(
                        out=S[dst][tt * 128:(tt + 1) * 128, dcol:dcol + 512], in_=vst[q][:]),
                        reads=[b_vst[q]], writes=[S["b_" + dst]], dsem=d_vst[q])
        P.flush()


class Sweep:
    def __init__(self, P, C, NQ, STR):
        self.P, self.C, self.NQ, self.STR = P, C, NQ, STR
        self.QW = NQ * 128
        self.PT = [C.sb([128, self.QW], BF16) for _ in range(3)]
        self.b_PT = [Buf() for _ in range(3)]
        self.osets = [C.ps([128, NQ, STR]) for _ in range(2)]
        self.b_oset = [Buf(excl=True), Buf(excl=True)]
        self.s_rot = 0
        self.p_rot = 0
        self.o_rot = 0

    def run(self, W, q_ap, q_bufs, nk, kt_list, K_fn, V_fn, bias_fn, qs_fn, first_fn, last_fn, done_fn):
        P, C, QW = self.P, self.C, self.QW
        os_ = self.o_rot % 2
        self.o_rot += 1
        oset = self.osets[os_]
        started = set()
        for kt in kt_list:
            sbk = self.s_rot % 3
            self.s_rot += 1
            k_ap, k_bufs = K_fn(kt)
            biases = bias_fn(kt)
            P.op("pe", lambda e, sbk=sbk, k_ap=k_ap, nb=len(biases): e.matmul(
                C.banks[sbk][0:nk, 0:QW], lhsT=k_ap, rhs=q_ap, start=True, stop=(nb == 0)),
                reads=list(k_bufs) + list(q_bufs), writes=[C.b_bank[sbk]])
            for j, (l_ap, r_ap, bfs) in enumerate(biases):
                P.op("pe", lambda e, sbk=sbk, l_ap=l_ap, r_ap=r_ap, last=(j == len(biases) - 1): e.matmul(
                    C.banks[sbk][0:nk, 0:QW], lhsT=l_ap, rhs=r_ap, start=False, stop=last),
                    reads=bfs, writes=[C.b_bank[sbk]])
            pt = self.p_rot % 3
            self.p_rot += 1
            P.op("act", lambda e, sbk=sbk, pt=pt: e.activation(
                out=self.PT[pt][0:nk, :], in_=C.banks[sbk][0:nk, 0:QW], func=AF.Exp, scale=SCALE),
                reads=[C.b_bank[sbk]], writes=[self.b_PT[pt]])
            v_ap, v_bufs = V_fn(kt)
            for qs in qs_fn(kt):
                bk = (qs * self.STR) // 512
                st = bk not in started
                started.add(bk)
                P.op("pe", lambda e, pt=pt, qs=qs, v_ap=v_ap, st=st, sp_=(kt == last_fn(qs)): e.matmul(
                    oset[:, qs, 0:W], lhsT=self.PT[pt][0:nk, qs * 128:(qs + 1) * 128], rhs=v_ap, start=st, stop=sp_,
                    skip_group_check=True),
                    reads=[self.b_PT[pt]] + list(v_bufs), writes=[self.b_oset[os_]])
        done_fn(oset, self.b_oset[os_])


def emit_nsa(P, nc, g, S, L, K):
    with Ctx(nc, P, nbf=1, nbanks=3) as C:
        QT = C.sb([128, 4, T], BF16)
        KcT, KsT, KwT, VcT = [C.sb([128, T], BF16) for _ in range(4)]
        Vs1 = C.sb([128, NTT, 129], BF16)
        Vw1 = C.sb([128, NTT, 129], BF16)
        gates = C.sb([128, NTT, 24], F32)
        posk = C.sb([128, 32], F32)
        posv = C.sb([128, 32], F32)
        w1k = C.sb([128, 32, 256], BF16)
        w1v = C.sb([128, 32, 256], BF16)
        w2k = C.sb([128, 2, 128], BF16)
        w2v = C.sb([128, 2, 128], BF16)
        tmp = [C.sb([128, 127], BF16) for _ in range(2)]
        hidT = C.sb([128, 2, 127], BF16)
        kcT = C.sb([128, 127], BF16)
        vc1 = C.sb([128, 161], BF16)
        acc = C.sb([128, NTT, 4, 128], F32)
        imp = C.sb([128, NTT, 32], F32)
        selbT = C.sb([32, T], BF16)
        ident = C.sb([128, 128], BF16)
        causb = C.sb([128, 4, 512], BF16)
        winb = C.sb([128, 4, 512], BF16)
        cmpb = C.sb([128, T], BF16)
        Eexp = C.sb([32, NTT, 128], BF16)
        impA = C.sb([128, NTT, 32], F32)
        impB = C.sb([128, NTT, 32], F32)
        b_Q = [Buf() for _ in range(4)]
        b_Kc, b_Ks, b_Kw, b_Vc, b_Vs, b_Vw, b_g = [Buf() for _ in range(7)]
        b_pos, b_w1k, b_w1v, b_w2, b_hid, b_kcT, b_vc1 = [Buf() for _ in range(7)]
        b_tmp = [Buf(), Buf()]
        b_acc = [[Buf() for _ in range(4)] for _ in range(NTT)]
        b_imp = [Buf() for _ in range(NTT)]
        b_selT = [Buf() for _ in range(4)]
        b_const = Buf()
        d_ld = P.new_dsem()
        d_w = P.new_dsem()
        fm = S["fm"]

        def loads(e):
            r = []
            r.append(e.dma_start(out=QT[:], in_=fm[4 * g:4 * g + 4].rearrange("h p t -> p h t")))
            r.append(e.dma_start(out=KcT[:], in_=fm[8 + g]))
            r.append(e.dma_start(out=KsT[:], in_=fm[10 + g]))
            r.append(e.dma_start(out=KwT[:], in_=fm[12 + g]))
            r.append(e.dma_start(out=VcT[:], in_=fm[30 + g]))
            r.append(e.dma_start(out=Vs1[:, :, 0:128], in_=S["vsw"][:, g * 128:(g + 1) * 128].rearrange("(k p) d -> p k d", p=128)))
            r.append(e.dma_start(out=Vw1[:, :, 0:128], in_=S["vsw"][:, 256 + g * 128:256 + (g + 1) * 128].rearrange("(k p) d -> p k d", p=128)))
            r.append(e.dma_start(out=gates[:], in_=S["gates"].rearrange("(k p) c -> p k c", p=128)))
            r.append(e.dma_start(out=posk[:], in_=L["pos_kT"]))
            r.append(e.dma_start(out=posv[:], in_=L["pos_vT"]))
            r.append(e.dma_start(out=ident[:], in_=K["ident"]))
            r.append(e.dma_start(out=causb[:], in_=K["causb"]))
            r.append(e.dma_start(out=winb[:], in_=K["winb"]))
            r.append(e.dma_start(out=cmpb[:], in_=K["cmpb"]))
            r.append(e.dma_start(out=Eexp[:], in_=K["Eexp"]))
            r.append(e.dma_start(out=impA[:], in_=K["impA"]))
            r.append(e.dma_start(out=impB[:], in_=K["impB"]))
            r.append(e.dma_start(out=vc1[:, 129:161], in_=K["ovl"]))
            return r
        src_bufs = [S["b_fm"][i] for i in (4 * g, 4 * g + 1, 4 * g + 2, 4 * g + 3, 8 + g, 10 + g, 12 + g, 30 + g)] + [S["b_vsw"], S["b_gates"]]
        all_ld = b_Q + [b_Kc, b_Ks, b_Kw, b_Vc, b_Vs, b_Vw, b_g, b_pos, b_const, b_vc1]
        P.op("sp", loads, reads=src_bufs, writes=all_ld, dsem=d_ld, ndma=18)

        def wloads(e):
            return [e.dma_start(out=w1k[:], in_=L["wk1"].rearrange("(l p) j -> p l j", p=128)),
                    e.dma_start(out=w1v[:], in_=L["wv1"].rearrange("(l p) j -> p l j", p=128)),
                    e.dma_start(out=w2k[:], in_=L["wk2"].rearrange("(c p) d -> p c d", p=128)),
                    e.dma_start(out=w2v[:], in_=L["wv2"].rearrange("(c p) d -> p c d", p=128))]
        P.op("pool", wloads, writes=[b_w1k, b_w1v, b_w2], dsem=d_w, ndma=4)
        P.op("pool", lambda e: e.memset(Vs1[:, :, 128:129], 1.0), writes=[b_Vs])
        P.op("pool", lambda e: e.memset(Vw1[:, :, 128:129], 1.0), writes=[b_Vw])
        P.op("pool", lambda e: e.memset(vc1[:, 128:129], 1.0), writes=[b_vc1])

        tc_ = 0
        for which in ("k", "v"):
            srcT, b_src, pos, w1, b_w1, w2 = (KcT, b_Kc, posk, w1k, b_w1k, w2k) if which == "k" else (VcT, b_Vc, posv, w1v, b_w1v, w2v)
            for l in range(32):
                tq = tc_ % 2
                tc_ += 1
                P.op("dve", lambda e, tq=tq, l=l, srcT=srcT, pos=pos: e.tensor_scalar_add(
                    out=tmp[tq][:], in0=srcT[:, l:l + 2017:16], scalar1=pos[:, l:l + 1]),
                    reads=[b_src, b_pos], writes=[b_tmp[tq]])
                for jc in range(2):
                    P.op("pe", lambda e, tq=tq, l=l, jc=jc, w1=w1: e.matmul(
                        C.banks[jc][:, 0:127], lhsT=w1[:, l, jc * 128:(jc + 1) * 128], rhs=tmp[tq][:],
                        start=(l == 0), stop=(l == 31)),
                        reads=[b_w1, b_tmp[tq]], writes=[C.b_bank[jc]])
            for jc in range(2):
                P.op("act", lambda e, jc=jc: e.activation(out=hidT[:, jc, :], in_=C.banks[jc][:, 0:127], func=AF.Silu),
                     reads=[C.b_bank[jc]], writes=[b_hid])
            if which == "k":
                for jc in range(2):
                    P.op("pe", lambda e, jc=jc: e.matmul(C.banks[2][:, 0:127], lhsT=w2k[:, jc, :], rhs=hidT[:, jc, :],
                                                         start=(jc == 0), stop=(jc == 1)),
                         reads=[b_w2, b_hid], writes=[C.b_bank[2]])
                P.op("act", lambda e: e.activation(out=kcT[:], in_=C.banks[2][:, 0:127], func=AF.Copy),
                     reads=[C.b_bank[2]], writes=[b_kcT])
            else:
                for jc in range(2):
                    P.op("pe", lambda e, jc=jc: e.matmul(C.banks[2][0:127, 0:128], lhsT=hidT[:, jc, :], rhs=w2v[:, jc, :],
                                                         start=(jc == 0), stop=(jc == 1)),
                         reads=[b_w2, b_hid], writes=[C.b_bank[2]])
                P.op("act", lambda e: e.activation(out=vc1[0:127, 0:128], in_=C.banks[2][0:127, 0:128], func=AF.Copy),
                     reads=[C.b_bank[2]], writes=[b_vc1])

        sw = Sweep(P, C, 4, 256)
        rds = [C.sb([128, 4], F32) for _ in range(4)]
        cfs = [C.sb([128, 4], F32) for _ in range(4)]
        tmps = [C.sb([128, 4, 128], F32) for _ in range(2)]
        itmp = [C.sb([128, 4, 32], F32) for _ in range(2)]
        b_rd = [Buf() for _ in range(4)]
        b_cf = [Buf() for _ in range(4)]
        b_tmp2 = [Buf(), Buf()]
        b_itmp = [Buf(), Buf()]
        rc = [0]
        tcn = [0]

        def evac(branch, r, qt, O, bO):
            h = 4 * g + r
            k = rc[0] % 4
            rc[0] += 1
            tts = slice(4 * qt, 4 * qt + 4)
            accb = [b_acc[tt][r] for tt in range(4 * qt, 4 * qt + 4)]
            impb = b_imp[4 * qt:4 * qt + 4]
            P.op("dve", lambda e, k=k, O=O: e.tensor_scalar_max(out=rds[k][:], in0=O[:, :, 128], scalar1=1e-30),
                 reads=[bO], writes=[b_rd[k]])
            P.op("dve", lambda e, k=k: e.reciprocal(out=rds[k][:], in_=rds[k][:]), reads=[b_rd[k]], writes=[b_rd[k]])
            P.op("dve", lambda e, k=k, c=3 * h + branch: e.tensor_tensor(
                out=cfs[k][:], in0=rds[k][:], in1=gates[:, tts, c], op=ALU.mult),
                reads=[b_rd[k], b_g], writes=[b_cf[k]])
            if branch == 0:
                P.op("dve", lambda e, k=k, r=r, O=O: e.tensor_tensor(
                    out=acc[:, tts, r, :], in0=O[:, :, 0:128], in1=cfs[k][:].unsqueeze(2).to_broadcast([128, 4, 128]), op=ALU.mult),
                    reads=[bO, b_cf[k]], writes=accb)
                if r == 0:
                    P.op("dve", lambda e, k=k, O=O: e.tensor_tensor(
                        out=imp[:, tts, :], in0=O[:, :, 129:161], in1=rds[k][:].unsqueeze(2).to_broadcast([128, 4, 32]), op=ALU.mult),
                        reads=[bO, b_rd[k]], writes=impb)
                else:
                    j = tcn[0] % 2
                    tcn[0] += 1
                    P.op("dve", lambda e, k=k, j=j, O=O: e.tensor_tensor(
                        out=itmp[j][:], in0=O[:, :, 129:161], in1=rds[k][:].unsqueeze(2).to_broadcast([128, 4, 32]), op=ALU.mult),
                        reads=[bO, b_rd[k]], writes=[b_itmp[j]])
                    P.op("pool", lambda e, j=j: e.tensor_tensor(out=imp[:, tts, :], in0=imp[:, tts, :], in1=itmp[j][:], op=ALU.add),
                         reads=[b_itmp[j]] + impb, writes=impb)
            else:
                j = tcn[0] % 2
                tcn[0] += 1
                P.op("dve", lambda e, k=k, j=j, O=O: e.tensor_tensor(
                    out=tmps[j][:], in0=O[:, :, 0:128], in1=cfs[k][:].unsqueeze(2).to_broadcast([128, 4, 128]), op=ALU.mult),
                    reads=[bO, b_cf[k]], writes=[b_tmp2[j]])
                P.op("pool", lambda e, j=j, r=r: e.tensor_tensor(out=acc[:, tts, r, :], in0=acc[:, tts, r, :], in1=tmps[j][:], op=ALU.add),
                     reads=[b_tmp2[j]] + accb, writes=accb)

        for r in range(4):
            for qt in range(4):
                q0 = qt * 512
                sw.run(161, QT[:, r, q0:q0 + 512], [b_Q[r]], 127, [0],
                       lambda kt: (kcT[:, 0:127], [b_kcT]),
                       lambda kt: (vc1[0:127, :], [b_vc1]),
                       lambda kt, q0=q0: [(ident[0:127, 0:127], cmpb[0:127, q0:q0 + 512], [b_const])],
                       lambda kt: range(4), lambda qs: 0, lambda qs: 0,
                       lambda O, bO, r=r, qt=qt: evac(0, r, qt, O, bO))
        impf = [C.sb([128, 32], F32) for _ in range(2)]
        wk1_ = [C.sb([128, 32], F32) for _ in range(2)]
        wk2_ = [C.sb([128, 32], F32) for _ in range(2)]
        m8 = [C.sb([128, 8], F32) for _ in range(2)]
        selb = [C.sb([128, 32], BF16) for _ in range(2)]
        b_sel = [Buf(), Buf()]
        for tt in range(NTT):
            k = tt % 2
            bs = b_sel[k]
            P.op("dve", lambda e, k=k, tt=tt: e.tensor_tensor(out=impf[k][:], in0=imp[:, tt, :], in1=impA[:, tt, :], op=ALU.mult),
                 reads=[b_imp[tt], b_const], writes=[bs])
            P.op("dve", lambda e, k=k, tt=tt: e.tensor_tensor(out=impf[k][:], in0=impf[k][:], in1=impB[:, tt, :], op=ALU.add),
                 reads=[bs, b_const], writes=[bs])
            P.op("dve", lambda e, k=k: e.max(out=m8[k][:], in_=impf[k][:]), reads=[bs], writes=[bs])
            P.op("dve", lambda e, k=k: e.match_replace(out=wk1_[k][:], in_to_replace=m8[k][:], in_values=impf[k][:], imm_value=-2.0),
                 reads=[bs], writes=[bs])
            P.op("dve", lambda e, k=k: e.max(out=m8[k][:], in_=wk1_[k][:]), reads=[bs], writes=[bs])
            P.op("dve", lambda e, k=k: e.match_replace(out=wk2_[k][:], in_to_replace=m8[k][:], in_values=wk1_[k][:], imm_value=-2.0),
                 reads=[bs], writes=[bs])
            P.op("dve", lambda e, k=k: e.tensor_sub(out=wk1_[k][:], in0=impf[k][:], in1=wk2_[k][:]), reads=[bs], writes=[bs])
            P.op("dve", lambda e, k=k: e.tensor_scalar_min(out=wk1_[k][:], in0=wk1_[k][:], scalar1=1.0), reads=[bs], writes=[bs])
            P.op("dve", lambda e, k=k: e.scalar_tensor_tensor(out=wk2_[k][:], in0=impf[k][:], scalar=0.0, in1=wk1_[k][:],
                                                              op0=ALU.is_ge, op1=ALU.mult), reads=[bs], writes=[bs])
            P.op("dve", lambda e, k=k: e.tensor_scalar(out=selb[k][:], in0=wk2_[k][:], scalar1=-1.0, scalar2=-NEGB,
                                                       op0=ALU.add, op1=ALU.mult), reads=[bs], writes=[bs])
            P.op("pe", lambda e, k=k: e.transpose(out=C.bfbanks[0][0:32, 0:128], in_=selb[k][:], identity=ident[:]),
                 reads=[bs, b_const], writes=[C.b_bfbank[0]])
            P.op("act", lambda e, tt=tt: e.activation(out=selbT[:, tt * 128:(tt + 1) * 128], in_=C.bfbanks[0][0:32, 0:128], func=AF.Copy),
                 reads=[C.b_bfbank[0]], writes=[b_selT[tt // 4]])
        for r in range(4):
            for qt in range(4):
                q0 = qt * 512

                def slc_bias(kt, qt=qt, q0=q0):
                    b = [(Eexp[:, kt, :], selbT[:, q0:q0 + 512], [b_const, b_selT[qt]])]
                    if kt >= 4 * qt:
                        b.append((ident[:], causb[:, kt - 4 * qt, :], [b_const]))
                    return b
                sw.run(129, QT[:, r, q0:q0 + 512], [b_Q[r]], 128, list(range(0, 4 * qt + 4)),
                       lambda kt: (KsT[:, kt * 128:(kt + 1) * 128], [b_Ks]),
                       lambda kt: (Vs1[:, kt, :], [b_Vs]),
                       slc_bias,
                       lambda kt, qt=qt: [qs for qs in range(4) if 4 * qt + qs >= kt],
                       lambda qs: 0, lambda qs, qt=qt: 4 * qt + qs,
                       lambda O, bO, r=r, qt=qt: evac(1, r, qt, O, bO))

                def win_bias(kt, qt=qt):
                    j = kt - (4 * qt - 4)
                    if j < 4:
                        return [(ident[:], winb[:, j, :], [b_const])]
                    return [(ident[:], causb[:, j - 4, :], [b_const])]
                sw.run(129, QT[:, r, q0:q0 + 512], [b_Q[r]], 128, list(range(max(0, 4 * qt - 4), 4 * qt + 4)),
                       lambda kt: (KwT[:, kt * 128:(kt + 1) * 128], [b_Kw]),
                       lambda kt: (Vw1[:, kt, :], [b_Vw]),
                       win_bias,
                       lambda kt, qt=qt: [qs for qs in range(4) if 4 * qt + qs - 4 <= kt <= 4 * qt + qs],
                       lambda qs, qt=qt: max(0, 4 * qt + qs - 4), lambda qs, qt=qt: 4 * qt + qs,
                       lambda O, bO, r=r, qt=qt: evac(2, r, qt, O, bO))
        aost = [C.sb([128, 512], BF16) for _ in range(2)]
        b_ao = [Buf(), Buf()]
        d_ao = [P.new_dsem(), P.new_dsem()]
        for tt in range(NTT):
            k = tt % 2
            P.op("act", lambda e, k=k, tt=tt: e.activation(out=aost[k][:], in_=acc[:, tt, :, :].rearrange("p h d -> p (h d)"), func=AF.Copy),
                 reads=b_acc[tt], writes=[b_ao[k]])
            P.op("sp", lambda e, k=k, tt=tt: e.dma_start(out=S["ao"][tt * 128:(tt + 1) * 128, g * 512:(g + 1) * 512], in_=aost[k][:]),
                 reads=[b_ao[k]], writes=[S["b_ao"]], dsem=d_ao[k])
        P.flush()


def emit_diff(P, nc, S, L, K, lam_init):
    with Ctx(nc, P, nbf=0, nbanks=3) as C:
        QdT = C.sb([128, 8, T], BF16)
        KdT = C.sb([128, 8, T], BF16)
        Vd1 = C.sb([128, NTT, 4, 257], BF16)
        ident = C.sb([128, 128], BF16)
        causb = C.sb([128, 4, 512], BF16)
        lamv = C.sb([128, 4, 128], F32)
        subg = C.sb([128, 256], F32)
        ltmp = C.sb([128, 2, 128], F32)
        ls = C.sb([128, 2], F32)
        nl = C.sb([128, 1], F32)
        a_t = [C.sb([128, 2, 256], F32) for _ in range(2)]
        junk = C.sb([128, 2, 256], F32)
        tmpd = C.sb([128, 2, 256], F32)
        b_Q, b_K, b_V, b_const, b_lam, b_nl = [Buf() for _ in range(6)]
        b_a = [Buf(), Buf()]
        b_junk = Buf()
        b_tmpd = Buf()
        d_ld = P.new_dsem()
        fm = S["fm"]

        def loads(e):
            return [e.dma_start(out=QdT[:], in_=fm[14:22].rearrange("h p t -> p h t")),
                    e.dma_start(out=KdT[:], in_=fm[22:30].rearrange("h p t -> p h t")),
                    e.dma_start(out=Vd1[:, :, 0, 0:256], in_=S["vd"][:, 0:256].rearrange("(k p) d -> p k d", p=128)),
                    e.dma_start(out=Vd1[:, :, 1, 0:256], in_=S["vd"][:, 256:512].rearrange("(k p) d -> p k d", p=128)),
                    e.dma_start(out=Vd1[:, :, 2, 0:256], in_=S["vd"][:, 512:768].rearrange("(k p) d -> p k d", p=128)),
                    e.dma_start(out=Vd1[:, :, 3, 0:256], in_=S["vd"][:, 768:1024].rearrange("(k p) d -> p k d", p=128)),
                    e.dma_start(out=ident[:], in_=K["ident"]),
                    e.dma_start(out=causb[:], in_=K["causb"]),
                    e.dma_start(out=lamv[:], in_=L["lamv"]),
                    e.dma_start(out=subg[:], in_=L["subg"])]
        P.op("sp", loads, reads=[S["b_fm"][i] for i in range(14, 30)] + [S["b_vd"]],
             writes=[b_Q, b_K, b_V, b_const, b_lam], dsem=d_ld, ndma=10)
        P.op("pool", lambda e: e.memset(Vd1[:, :, :, 256:257], 1.0), writes=[b_V])
        P.op("dve", lambda e: e.tensor_tensor(out=ltmp[:, 0, :], in0=lamv[:, 0, :], in1=lamv[:, 1, :], op=ALU.mult),
             reads=[b_lam], writes=[b_nl])
        P.op("dve", lambda e: e.tensor_tensor(out=ltmp[:, 1, :], in0=lamv[:, 2, :], in1=lamv[:, 3, :], op=ALU.mult),
             reads=[b_lam, b_nl], writes=[b_nl])
        P.op("dve", lambda e: e.reduce_sum(out=ls[:], in_=ltmp[:], axis=AX.X), reads=[b_nl], writes=[b_nl])
        P.op("act", lambda e: e.activation(out=ls[:], in_=ls[:], func=AF.Exp), reads=[b_nl], writes=[b_nl])
        P.op("dve", lambda e: e.tensor_sub(out=nl[:], in0=ls[:, 1:2], in1=ls[:, 0:1]), reads=[b_nl], writes=[b_nl])
        P.op("dve", lambda e: e.tensor_scalar_add(out=nl[:], in0=nl[:], scalar1=-lam_init), reads=[b_nl], writes=[b_nl])

        sw = Sweep(P, C, 2, 512)
        rds = [C.sb([128, 2], F32) for _ in range(4)]
        b_rd = [Buf() for _ in range(4)]
        ost = [C.sb([128, 2, 256], BF16) for _ in range(2)]
        b_ost = [Buf(), Buf()]
        d_ost = [P.new_dsem(), P.new_dsem()]
        rc = [0]
        ac = [0]

        def bc(ap2):
            return ap2.unsqueeze(2).to_broadcast([128, 2, 256])

        def evac(hd, c, q2, O, bO):
            k = rc[0] % 4
            rc[0] += 1
            P.op("dve", lambda e, k=k, O=O: e.tensor_scalar_max(out=rds[k][:], in0=O[:, :, 256], scalar1=1e-30),
                 reads=[bO], writes=[b_rd[k]])
            P.op("dve", lambda e, k=k: e.reciprocal(out=rds[k][:], in_=rds[k][:]), reads=[b_rd[k]], writes=[b_rd[k]])
            if c == 0:
                ac[0] += 1
                ai = ac[0] % 2
                P.op("dve", lambda e, k=k, ai=ai, O=O: e.tensor_tensor(out=a_t[ai][:], in0=O[:, :, 0:256], in1=bc(rds[k][:]), op=ALU.mult),
                     reads=[bO, b_rd[k]], writes=[b_a[ai]])
                return
            ai = ac[0] % 2
            at = a_t[ai]
            P.op("dve", lambda e, k=k: e.tensor_scalar_mul(out=rds[k][:], in0=rds[k][:], scalar1=nl[:, 0:1]),
                 reads=[b_rd[k], b_nl], writes=[b_rd[k]])
            P.op("dve", lambda e, k=k, O=O: e.tensor_tensor(out=tmpd[:], in0=O[:, :, 0:256], in1=bc(rds[k][:]), op=ALU.mult),
                 reads=[bO, b_rd[k]], writes=[b_tmpd])
            P.op("pool", lambda e, at=at: e.tensor_tensor(out=at[:], in0=at[:], in1=tmpd[:], op=ALU.add),
                 reads=[b_tmpd, b_a[ai]], writes=[b_a[ai]])
            P.op("pool", lambda e, at=at: e.tensor_tensor(out=junk[:], in0=at[:], in1=at[:], op=ALU.mult),
                 reads=[b_a[ai]], writes=[b_junk])
            P.op("dve", lambda e, k=k: e.reduce_sum(out=rds[k][:], in_=junk[:], axis=AX.X), reads=[b_junk], writes=[b_rd[k]])
            P.op("dve", lambda e, k=k: e.tensor_scalar(out=rds[k][:], in0=rds[k][:], scalar1=1.0 / 256, scalar2=EPS,
                                                       op0=ALU.mult, op1=ALU.add), reads=[b_rd[k]], writes=[b_rd[k]])
            P.op("act", lambda e, k=k: e.activation(out=rds[k][:], in_=rds[k][:], func=AF.Sqrt), reads=[b_rd[k]], writes=[b_rd[k]])
            P.op("dve", lambda e, k=k: e.reciprocal(out=rds[k][:], in_=rds[k][:]), reads=[b_rd[k]], writes=[b_rd[k]])
            P.op("dve", lambda e, k=k: e.tensor_scalar_mul(out=rds[k][:], in0=rds[k][:], scalar1=1.0 - lam_init),
                 reads=[b_rd[k]], writes=[b_rd[k]])
            o = q2 % 2
            P.op("dve", lambda e, k=k, at=at: e.tensor_tensor(out=tmpd[:], in0=at[:], in1=bc(rds[k][:]), op=ALU.mult),
                 reads=[b_a[ai], b_rd[k]], writes=[b_tmpd])
            P.op("pool", lambda e, o=o: e.tensor_tensor(out=ost[o][:], in0=tmpd[:], in1=subg[:].unsqueeze(1).to_broadcast([128, 2, 256]), op=ALU.mult),
                 reads=[b_tmpd, b_lam], writes=[b_ost[o]])
            P.op("sp", lambda e, o=o, q2=q2, hd=hd: e.dma_start(
                out=S["ao"][q2 * 256:(q2 + 1) * 256, 1024 + hd * 256:1024 + (hd + 1) * 256].rearrange("(k p) d -> p k d", p=128),
                in_=ost[o][:]),
                reads=[b_ost[o]], writes=[S["b_ao"]], dsem=d_ost[o])

        for hd in range(4):
            for q2 in range(8):
                q0 = q2 * 256
                for c in range(2):
                    m = 2 * hd + c
                    sw.run(257, QdT[:, m, q0:q0 + 256], [b_Q], 128, list(range(0, 2 * q2 + 2)),
                           lambda kt, m=m: (KdT[:, m, kt * 128:(kt + 1) * 128], [b_K]),
                           lambda kt, hd=hd: (Vd1[:, kt, hd, :], [b_V]),
                           lambda kt, q2=q2: ([(ident[:], causb[:, kt - 2 * q2, 0:256], [b_const])] if kt >= 2 * q2 else []),
                           lambda kt, q2=q2: [qs for qs in range(2) if 2 * q2 + qs >= kt],
                           lambda qs: 0, lambda qs, q2=q2: 2 * q2 + qs,
                           lambda O, bO, hd=hd, c=c, q2=q2: evac(hd, c, q2, O, bO))
        P.flush()


def emit_wout(P, nc, xT, xT_bufs, S, w_out, K):
    with Ctx(nc, P, nbf=2) as C:
        AOT = C.sb([128, NDC, T], BF16)
        ident = C.sb([128, 128], BF16)
        aot = [C.sb([128, 2048], BF16) for _ in range(2)]
        b_AOT = [Buf() for _ in range(4)]
        b_aot = [Buf(), Buf()]
        b_id = Buf()
        d_aot = [P.new_dsem(), P.new_dsem()]
        d_id = P.new_dsem()
        P.op("sp", lambda e: e.dma_start(out=ident[:], in_=K["ident"]), writes=[b_id], dsem=d_id)
        tc_ = 0
        for tt in range(NTT):
            k = tt % 2
            P.op("sp", lambda e, k=k, tt=tt: e.dma_start(out=aot[k][:], in_=S["ao"][tt * 128:(tt + 1) * 128, :]),
                 reads=[S["b_ao"]], writes=[b_aot[k]], dsem=d_aot[k])
            for half in range(2):
                bb = tc_ % 2
                tc_ += 1
                for j in range(8):
                    cc = half * 8 + j
                    P.op("pe", lambda e, k=k, cc=cc, j=j, bb=bb: e.transpose(
                        out=C.bfbanks[bb][:, j * 128:(j + 1) * 128], in_=aot[k][:, cc * 128:(cc + 1) * 128], identity=ident[:]),
                        reads=[b_aot[k], b_id], writes=[C.b_bfbank[bb]])
                P.op("act", lambda e, half=half, tt=tt, bb=bb: e.activation(
                    out=AOT[:, half * 8:(half + 1) * 8, tt * 128:(tt + 1) * 128],
                    in_=C.bfbanks[bb][:, :].rearrange("p (c t) -> p c t", c=8), func=AF.Copy),
                    reads=[C.b_bfbank[bb]], writes=[b_AOT[tt // 4]])
        emit_down_cached(P, C, xT, xT_bufs, w_out, NDC,
                         lambda c, tg: (AOT[:, c, tg * 512:(tg + 1) * 512], b_AOT[tg]),
                         [0, 512, 1024, 1536], 1.0)
        P.flush()


def emit_final(P, nc, xT, xT_bufs, g_pc, outT):
    xT_v = xT.rearrange("(c p) t -> p c t", p=128)
    with Ctx(nc, P) as C:
        ost = [C.sb([128, 256], F32) for _ in range(4)]
        b_ost = [Buf() for _ in range(4)]
        d_ost = [P.new_dsem() for _ in range(4)]
        cnt = [0]

        def out_fn(dc, ta, NG):
            k = cnt[0] % 4
            return ost[k][:], [b_ost[k]]

        def post_fn(dc, ta, NG):
            k = cnt[0] % 4
            cnt[0] += 1
            P.op("sp", lambda e, k=k, dc=dc, ta=ta, NG=NG: e.dma_start(out=outT[dc * 128:(dc + 1) * 128, ta:ta + NG], in_=ost[k][:]),
                 reads=[b_ost[k]], dsem=d_ost[k])
        emit_norm(P, C, xT_v, lambda ta: [xT_bufs[dc][ta // 512] for dc in range(NDC)], g_pc, 0, T, out_fn, post_fn=post_fn)
        P.flush()


LAYER_INPUTS = [
    ("n1", [128, NDC], F32), ("wg1", [D, DFF], F32), ("wu1", [D, DFF], F32), ("wd1", [DFF, D], F32),
    ("nm", [128, NDC], F32), ("w_in", [D, IN_WIDTH], F32),
    ("pos_kT", [128, 32], F32), ("pos_vT", [128, 32], F32),
    ("wk1", [4096, 256], F32), ("wk2", [256, 128], F32), ("wv1", [4096, 256], F32), ("wv2", [256, 128], F32),
    ("lamv", [128, 4, 128], F32), ("subg", [128, 256], F32), ("w_out", [D, D], F32),
    ("n2", [128, NDC], F32), ("wg2", [D, DFF], F32), ("wu2", [D, DFF], F32), ("wd2", [DFF, D], F32),
]
CONST_INPUTS = [
    ("cosT", [128, T], F32), ("sinT", [128, T], F32), ("ident", [128, 128], BF16),
    ("causb", [128, 4, 512], BF16), ("winb", [128, 4, 512], BF16), ("cmpb", [128, T], BF16),
    ("Eexp", [32, NTT, 128], BF16), ("impA", [128, NTT, 32], F32), ("impB", [128, NTT, 32], F32),
    ("ovl", [128, 32], BF16), ("permsw", [128, 128], BF16),
]


def build(n_layers=2, parts=("ffn1", "mix", "ffn2"), final=True, debug_scratch=False):
    nc = bass.Bass("TRN2", target_bir_lowering=False)
    xin = nc.dram_tensor("xT_in", [D, T], F32, kind="ExternalInput").ap()
    Ls = []
    for l in range(n_layers):
        Ls.append({n: nc.dram_tensor(f"{n}_{l}", shp, dt, kind="ExternalInput").ap() for n, shp, dt in LAYER_INPUTS})
    K = {n: nc.dram_tensor(n, shp, dt, kind="ExternalInput").ap() for n, shp, dt in CONST_INPUTS}
    nf = nc.dram_tensor("nf", [128, NDC], F32, kind="ExternalInput").ap()
    outT = nc.dram_tensor("outT", [D, T], F32, kind="ExternalOutput").ap()
    xT = nc.dram_tensor("xT", [D, T], F32).ap()
    kind = "ExternalOutput" if debug_scratch else "Internal"
    S = {
        "fm": nc.dram_tensor("s_fm", [32, 128, T], BF16, kind=kind).ap(),
        "vsw": nc.dram_tensor("s_vsw", [T, 512], BF16, kind=kind).ap(),
        "vd": nc.dram_tensor("s_vd", [T, 1024], BF16, kind=kind).ap(),
        "gates": nc.dram_tensor("s_gates", [T, 24], F32, kind=kind).ap(),
        "ao": nc.dram_tensor("s_ao", [T, 2048], BF16, kind=kind).ap(),
        "b_fm": [Buf() for _ in range(32)], "b_vsw": Buf(), "b_vd": Buf(), "b_gates": Buf(), "b_ao": Buf(),
    }
    with ExitStack() as es:
        P = Prog(nc, SemPool(nc, es))
        xb = [[Buf() for _ in range(4)] for _ in range(NDC)]
        d0 = P.new_dsem()
        P.op("sp", lambda e: e.dma_start(out=xT, in_=xin), writes=[b for r in xb for b in r], dsem=d0)
        P.flush()
        for l in range(n_layers):
            L = Ls[l]
            lam_init = 0.8 - 0.6 * math.exp(-0.3 * l)
            if "ffn1" in parts:
                with nc.named_scope(f"L{l}_ffn1"):
                    emit_ffn(P, nc, xT, xb, L["wg1"], L["wu1"], L["wd1"], L["n1"])
            if "mix" in parts:
                with nc.named_scope(f"L{l}_proj"):
                    emit_proj(P, nc, xT, xb, L["w_in"], L["nm"], K["cosT"], K["sinT"], S, K)
                with nc.named_scope(f"L{l}_nsa0"):
                    emit_nsa(P, nc, 0, S, L, K)
                with nc.named_scope(f"L{l}_nsa1"):
                    emit_nsa(P, nc, 1, S, L, K)
                with nc.named_scope(f"L{l}_diff"):
                    emit_diff(P, nc, S, L, K, lam_init)
                with nc.named_scope(f"L{l}_wout"):
                    emit_wout(P, nc, xT, xb, S, L["w_out"], K)
            if "ffn2" in parts:
                with nc.named_scope(f"L{l}_ffn2"):
                    emit_ffn(P, nc, xT, xb, L["wg2"], L["wu2"], L["wd2"], L["n2"])
        if final:
            emit_final(P, nc, xT, xb, nf, outT)
        else:
            d1 = P.new_dsem()
            P.op("sp", lambda e: e.dma_start(out=outT, in_=xT), reads=[b for r in xb for b in r], dsem=d1)
            P.flush()
    return nc


def _bf(a):
    import ml_dtypes
    return np.ascontiguousarray(a.astype(np.float32)).astype(ml_dtypes.bfloat16)


def make_consts():
    inv = 1.0 / (10000.0 ** (np.arange(0, HD, 2, dtype=np.float32) / HD))
    ang = np.arange(T, dtype=np.float32)[:, None] * inv[None, :]
    ang = np.concatenate([ang, ang], axis=-1)
    cos, sin = np.cos(ang).astype(np.float32), np.sin(ang).astype(np.float32)
    sgn = np.concatenate([-np.ones(64), np.ones(64)]).astype(np.float32)
    k = np.arange(128)[:, None, None]
    j = np.arange(4)[None, :, None]
    q = np.arange(512)[None, None, :]
    causb = np.where(k + 128 * j <= q, 0.0, NEGB)
    winb = np.where(q - k < 128 * j, 0.0, NEGB)
    n = np.arange(128)[:, None]
    t = np.arange(T)[None, :]
    cmpb = np.where(16 * n + 31 <= t, 0.0, NEGB)
    s = np.arange(32)[:, None, None]
    kt = np.arange(NTT)[None, :, None]
    kk = np.arange(128)[None, None, :]
    Eexp = (s == (kt * 128 + kk) // 64).astype(np.float32)
    tt = np.arange(T)
    cur = tt // 64
    blk = np.arange(32)[None, :]
    forced = (blk == 0) | (blk == cur[:, None]) | (blk == cur[:, None] - 1)
    causal = blk * 64 <= tt[:, None]
    A = ((~forced) & causal).astype(np.float32)
    B = (1e4 * (forced & causal) + (causal.astype(np.float32) - 1.0)).astype(np.float32)
    impA = np.ascontiguousarray(A.reshape(NTT, 128, 32).transpose(1, 0, 2))
    impB = np.ascontiguousarray(B.reshape(NTT, 128, 32).transpose(1, 0, 2))
    c0 = np.arange(127) * 16
    s0 = np.arange(32) * 64
    lo = np.maximum(c0[:, None], s0[None, :])
    hi = np.minimum(c0[:, None] + 32, s0[None, :] + 64)
    ov = np.zeros((128, 32), np.float32)
    ov[:127] = np.clip(hi - lo, 0, None) / 32.0
    return {
        "cosT": np.ascontiguousarray(cos.T), "sinT": np.ascontiguousarray((sin * sgn).T),
        "ident": _bf(np.eye(128)), "causb": _bf(causb), "winb": _bf(winb), "cmpb": _bf(cmpb),
        "Eexp": _bf(Eexp), "impA": impA, "impB": impB, "ovl": _bf(ov),
        "permsw": _bf(np.roll(np.eye(128), 64, axis=1)),
    }


def _pc(v):
    return np.ascontiguousarray(np.asarray(v, np.float32).reshape(NDC, 128).T)


def layer_inputs(l, p):
    c = np.ascontiguousarray
    return {
        f"n1_{l}": _pc(p["ffn1_norm"][l]), f"wg1_{l}": c(p["ffn1_w_gate"][l]), f"wu1_{l}": c(p["ffn1_w_up"][l]),
        f"wd1_{l}": c(p["ffn1_w_down"][l]),
        f"nm_{l}": _pc(p["mix_norm"][l]), f"w_in_{l}": c(p["w_in"][l]),
        f"pos_kT_{l}": c(np.asarray(p["cmp_pos_k"][l]).T), f"pos_vT_{l}": c(np.asarray(p["cmp_pos_v"][l]).T),
        f"wk1_{l}": c(p["cmp_wk1"][l]), f"wk2_{l}": c(p["cmp_wk2"][l]),
        f"wv1_{l}": c(p["cmp_wv1"][l]), f"wv2_{l}": c(p["cmp_wv2"][l]),
        f"lamv_{l}": c(np.broadcast_to(np.stack([np.asarray(p[k][l]) for k in ("lam_q1", "lam_k1", "lam_q2", "lam_k2")])[None],
                                       (128, 4, 128))),
        f"subg_{l}": c(np.broadcast_to(np.asarray(p["diff_subln"][l])[None], (128, 256))),
        f"w_out_{l}": c(p["w_out"][l]),
        f"n2_{l}": _pc(p["ffn2_norm"][l]), f"wg2_{l}": c(p["ffn2_w_gate"][l]), f"wu2_{l}": c(p["ffn2_w_up"][l]),
        f"wd2_{l}": c(p["ffn2_w_down"][l]),
    }


_NC_CACHE = {}


def kernel(**inputs):
    p = {k: np.asarray(v) for k, v in inputs.items()}
    x = p["x"]
    B = x.shape[0]
    nc = build(2)
    shared = dict(make_consts())
    for l in range(2):
        shared.update(layer_inputs(l, p))
    shared["nf"] = _pc(p["final_norm"])
    in_maps = []
    for b in range(B):
        m = dict(shared)
        m["xT_in"] = np.ascontiguousarray(x[b].T)
        in_maps.append(m)
    res = run_bass_kernel_spmd(nc, in_maps, core_ids=list(range(B)))
    out = np.stack([np.ascontiguousarray(res.results[b]["outT"].T) for b in range(B)], axis=0)
    return out.astype(np.float32)
```

```python
import math
from contextlib import ExitStack
import numpy as np
import concourse.bass as bass
import concourse.mybir as mybir
from concourse.bass_utils import run_bass_kernel_spmd

F32 = mybir.dt.float32
BF16 = mybir.dt.bfloat16
AF = mybir.ActivationFunctionType
ALU = mybir.AluOpType
AX = mybir.AxisListType

ENGS = ("pe", "act", "dve", "pool", "sp")
EPOCH = 12000
STRICT_SAME_ENGINE = True

D = 2048
NDC = 16
T = 2048
NTT = 16
DFF = 5632
HD = 128
EPS = 1e-6
NEGB = -30000.0
SCALE = HD ** -0.5
IN_WIDTH = 5656
C_QN, C_KC, C_VC, C_KS, C_VS, C_KW, C_VW, C_G, C_QD, C_KD, C_VD = (
    0, 1024, 1280, 1536, 1792, 2048, 2304, 2560, 2584, 3608, 4632)


class Buf:
    __slots__ = ("w", "rs", "excl")

    def __init__(self, excl=False):
        self.w = None
        self.rs = []
        self.excl = excl


class DSem:
    __slots__ = ("sem", "count")

    def __init__(self, sem):
        self.sem = sem
        self.count = 0


class Op:
    __slots__ = ("eng", "fn", "deps", "needs_inc", "tok", "dsem")


class SemPool:
    def __init__(self, nc, es):
        self.nc, self.es, self.n = nc, es, 0

    def pop(self):
        self.n += 1
        return self.es.enter_context(self.nc.semaphore(f"sem{self.n}"))


class Prog:
    def __init__(self, nc, sem_pool):
        self.nc = nc
        self.sem_pool = sem_pool
        self.ops = {e: [] for e in ENGS}
        self.ctr_sems = {e: [] for e in ENGS}
        self.ctr = {e: 0 for e in ENGS}
        self.waited = {e: {} for e in ENGS}
        self.dsems = []
        self.free_dsems = []

    def new_dsem(self):
        if self.free_dsems:
            d = self.free_dsems.pop()
        else:
            d = DSem(self.sem_pool.pop())
        self.dsems.append(d)
        return d

    def op(self, eng, fn, reads=(), writes=(), dsem=None, ndma=1):
        o = Op()
        o.eng = eng
        o.fn = fn
        o.needs_inc = False
        o.dsem = dsem
        deps = {}
        for b in reads:
            if b.w is not None:
                deps[id(b.w)] = b.w
            if b.excl:
                for r in b.rs:
                    if r.eng != eng:
                        deps[id(r)] = r
        for b in writes:
            if b.w is not None:
                deps[id(b.w)] = b.w
            for r in b.rs:
                deps[id(r)] = r
        dl = []
        for d in deps.values():
            if d.eng == eng and d.dsem is None:
                if eng == "pe" or not STRICT_SAME_ENGINE:
                    continue
            dl.append(d)
            d.needs_inc = True
        o.deps = dl
        for b in reads:
            if dsem is None:
                b.rs = [r for r in b.rs if not (r.eng == eng and r.dsem is None)]
            b.rs.append(o)
        for b in writes:
            b.w = o
            b.rs = []
        if dsem is not None:
            dsem.count += 16 * ndma
            o.tok = (dsem.sem, dsem.count)
        else:
            o.tok = None
        self.ops[eng].append(o)
        return o

    def _assign_tokens(self):
        for e in ENGS:
            for o in self.ops[e]:
                if o.dsem is None and o.needs_inc:
                    c = self.ctr[e]
                    ep = c // EPOCH
                    while len(self.ctr_sems[e]) <= ep:
                        self.ctr_sems[e].append(self.sem_pool.pop())
                    o.tok = (self.ctr_sems[e][ep], c % EPOCH + 1)
                    self.ctr[e] = c + 1

    def _emit_engine(self, e, engobj):
        waited = self.waited[e]
        for o in self.ops[e]:
            for d in o.deps:
                sem, val = d.tok
                k = id(sem)
                if waited.get(k, 0) < val:
                    engobj.wait_ge(sem, val)
                    waited[k] = val
            r = o.fn(engobj)
            if o.dsem is not None:
                if not isinstance(r, (list, tuple)):
                    r = [r]
                for ins in r:
                    ins.then_inc(o.dsem.sem, 16)
            elif o.needs_inc:
                r.then_inc(o.tok[0], 1)

    def flush(self):
        self._assign_tokens()
        nc = self.nc
        finals = [(d.sem, d.count) for d in self.dsems if d.count > 0]
        with nc.Block() as block:
            @block.tensor
            def _(eng):
                self._emit_engine("pe", eng)

            @block.scalar
            def _(eng):
                self._emit_engine("act", eng)

            @block.vector
            def _(eng):
                self._emit_engine("dve", eng)

            @block.gpsimd
            def _(eng):
                self._emit_engine("pool", eng)

            @block.sync
            def _(eng):
                self._emit_engine("sp", eng)
                for (sem, val) in finals:
                    eng.wait_ge(sem, val)
        self.ops = {e: [] for e in ENGS}
        self.free_dsems.extend(self.dsems)
        self.dsems = []


class Ctx:
    def __init__(self, nc, P, nbf=0, nbanks=None):
        self.nc = nc
        self.P = P
        self.es = ExitStack()
        self.n = 0
        self.bank_i = 0
        self.nbf = nbf
        self.nbanks = (8 - nbf) if nbanks is None else nbanks

    def __enter__(self):
        self.es.__enter__()
        Ctx.uid = getattr(Ctx, "uid", 0) + 1
        self.u = Ctx.uid
        self.banks = [self.es.enter_context(self.nc.psum_tensor(f"bk{i}_{self.u}", [128, 512], F32))
                      for i in range(self.nbanks)]
        self.bfbanks = [self.es.enter_context(self.nc.psum_tensor(f"bfk{i}_{self.u}", [128, 1024], BF16))
                        for i in range(self.nbf)]
        self.b_bank = [Buf(excl=True) for _ in range(8)]
        self.b_bfbank = [Buf(excl=True) for _ in range(self.nbf)]
        return self

    def __exit__(self, *a):
        return self.es.__exit__(*a)

    def sb(self, shape, dt):
        self.n += 1
        return self.es.enter_context(self.nc.sbuf_tensor(f"t{self.n}_{self.u}", shape, dt))

    def next_bank(self):
        i = self.bank_i
        self.bank_i = (i + 1) % self.nbanks
        return i

    def ps(self, shape, dt=F32):
        self.n += 1
        return self.es.enter_context(self.nc.psum_tensor(f"p{self.n}_{self.u}", shape, dt))


def emit_norm(P, C, xT_v, rd_bufs, g_pc, t_lo, t_hi, out_fn, NG=256, post_fn=None):
    st = getattr(C, "_norm", None)
    if st is None:
        st = {}
        C._norm = st
        st["ones"] = C.sb([128, 128], BF16)
        st["epsb"] = C.sb([128, 1], F32)
        st["gsb"] = C.sb([128, NDC], F32)
        st["xin"] = C.sb([128, NDC, NG], F32)
        st["sq"] = [C.sb([128, NG], BF16) for _ in range(2)]
        st["rstd"] = C.sb([128, NG], F32)
        st["b_c"], st["b_g"], st["b_xin"], st["b_rstd"] = Buf(), Buf(), Buf(), Buf()
        st["b_sq"] = [Buf(), Buf()]
        st["d_g"] = P.new_dsem()
        st["d_xin"] = P.new_dsem()
        st["sqc"] = 0
        P.op("pool", lambda e: e.memset(st["epsb"][:], EPS), writes=[st["b_c"]])
        P.op("pool", lambda e: e.memset(st["ones"][:], 1.0), writes=[st["b_c"]])
    ones, epsb, gsb, xin, sq, rstd = st["ones"], st["epsb"], st["gsb"], st["xin"], st["sq"], st["rstd"]
    b_c, b_g, b_xin, b_rstd, b_sq = st["b_c"], st["b_g"], st["b_xin"], st["b_rstd"], st["b_sq"]
    P.op("sp", lambda e: e.dma_start(out=gsb[:], in_=g_pc), writes=[b_g], dsem=st["d_g"])
    for ta in range(t_lo, t_hi, NG):
        P.op("sp", lambda e, ta=ta: e.dma_start(out=xin[:], in_=xT_v[:, :, ta:ta + NG]),
             reads=rd_bufs(ta), writes=[b_xin], dsem=st["d_xin"])
        bi = C.next_bank()
        for dc in range(NDC):
            s = st["sqc"] % 2
            st["sqc"] += 1
            P.op("act", lambda e, s=s, dc=dc: e.activation(out=sq[s][:], in_=xin[:, dc, :], func=AF.Square),
                 reads=[b_xin], writes=[b_sq[s]])
            P.op("pe", lambda e, s=s, dc=dc, bi=bi: e.matmul(C.banks[bi][:, 0:NG], lhsT=ones[:], rhs=sq[s][:],
                                                            start=(dc == 0), stop=(dc == NDC - 1)),
                 reads=[b_c, b_sq[s]], writes=[C.b_bank[bi]])
        P.op("act", lambda e, bi=bi: e.activation(out=rstd[:], in_=C.banks[bi][:, 0:NG], func=AF.Sqrt,
                                                  scale=1.0 / D, bias=epsb[:]),
             reads=[C.b_bank[bi], b_c], writes=[b_rstd])
        P.op("dve", lambda e: e.reciprocal(out=rstd[:], in_=rstd[:]), reads=[b_rstd], writes=[b_rstd])
        for dc in range(NDC):
            o_ap, o_bufs = out_fn(dc, ta, NG)
            P.op("dve", lambda e, dc=dc, o_ap=o_ap: e.scalar_tensor_tensor(
                out=o_ap, in0=xin[:, dc, :], scalar=gsb[:, dc:dc + 1], in1=rstd[:], op0=ALU.mult, op1=ALU.mult),
                reads=[b_xin, b_g, b_rstd], writes=o_bufs)
            if post_fn is not None:
                post_fn(dc, ta, NG)


def emit_ffn(P, nc, xT, xT_bufs, wg, wu, wd, g_pc, F=DFF, TT=1024, coef=0.5, Tn=T, xsrc=None):
    NFC = F // 128
    NTG = TT // 512
    xs = xT if xsrc is None else xsrc
    xT_v = xs.rearrange("(c p) t -> p c t", p=128)
    src_bufs = xT_bufs if xsrc is None else [[Buf() for _ in range(4)] for _ in range(NDC)]
    wg_v = wg.rearrange("(c p) f -> p c f", p=128)
    wu_v = wu.rearrange("(c p) f -> p c f", p=128)
    with Ctx(nc, P) as C:
        xnT = C.sb([128, NDC, TT], BF16)
        hT = C.sb([128, NFC, TT], BF16)
        NWS = 2
        wgs = [C.sb([128, NDC, 128], BF16) for _ in range(NWS)]
        wus = [C.sb([128, NDC, 128], BF16) for _ in range(NWS)]
        sg = [C.sb([128, 512], F32) for _ in range(2)]
        b_xnT = [[Buf() for _ in range(NTG)] for _ in range(NDC)]
        b_hT = [[Buf() for _ in range(NTG)] for _ in range(NFC)]
        b_wg = [Buf() for _ in wgs]
        b_wu = [Buf() for _ in wus]
        b_sg = [Buf() for _ in sg]
        d_wg = [P.new_dsem() for _ in wgs]
        d_wu = [P.new_dsem() for _ in wus]
        wcount = 0
        for h in range(Tn // TT):
            t0 = h * TT
            emit_norm(P, C, xT_v, lambda ta: [src_bufs[dc][ta // 512] for dc in range(NDC)], g_pc, t0, t0 + TT,
                      lambda dc, ta, NG, t0=t0: (xnT[:, dc, ta - t0:ta - t0 + NG], [b_xnT[dc][(ta - t0) // 512]]))
            for fc in range(NFC):
                s = wcount % NWS
                wcount += 1
                P.op("pool", lambda e, s=s, fc=fc: e.dma_start(out=wgs[s][:], in_=wg_v[:, :, fc * 128:(fc + 1) * 128]),
                     writes=[b_wg[s]], dsem=d_wg[s])
                P.op("pool", lambda e, s=s, fc=fc: e.dma_start(out=wus[s][:], in_=wu_v[:, :, fc * 128:(fc + 1) * 128]),
                     writes=[b_wu[s]], dsem=d_wu[s])
                for tg in range(NTG):
                    bg = C.next_bank()
                    bu = C.next_bank()
                    for dc in range(NDC):
                        P.op("pe", lambda e, s=s, dc=dc, tg=tg, bg=bg: e.matmul(
                            C.banks[bg][:], lhsT=wgs[s][:, dc, :], rhs=xnT[:, dc, tg * 512:(tg + 1) * 512],
                            start=(dc == 0), stop=(dc == NDC - 1)),
                            reads=[b_wg[s], b_xnT[dc][tg]], writes=[C.b_bank[bg]])
                    for dc in range(NDC):
                        P.op("pe", lambda e, s=s, dc=dc, tg=tg, bu=bu: e.matmul(
                            C.banks[bu][:], lhsT=wus[s][:, dc, :], rhs=xnT[:, dc, tg * 512:(tg + 1) * 512],
                            start=(dc == 0), stop=(dc == NDC - 1)),
                            reads=[b_wu[s], b_xnT[dc][tg]], writes=[C.b_bank[bu]])
                    q = (fc * NTG + tg) % 2
                    P.op("act", lambda e, q=q, bg=bg: e.activation(out=sg[q][:], in_=C.banks[bg][:], func=AF.Silu),
                         reads=[C.b_bank[bg]], writes=[b_sg[q]])
                    P.op("dve", lambda e, q=q, bu=bu, fc=fc, tg=tg: e.tensor_tensor(
                        out=hT[:, fc, tg * 512:(tg + 1) * 512], in0=sg[q][:], in1=C.banks[bu][:], op=ALU.mult),
                        reads=[b_sg[q], C.b_bank[bu]], writes=[b_hT[fc][tg]])
            emit_down_cached(P, C, xT, xT_bufs, wd, NFC,
                             lambda c, tg: (hT[:, c, tg * 512:(tg + 1) * 512], b_hT[c][tg]),
                             [t0 + i * 512 for i in range(NTG)], coef, xsrc=xsrc, src_bufs=src_bufs)
        P.flush()


def emit_down_cached(P, C, xT, xT_bufs, w, NC_, rhs_fn, toks, coef, xsrc=None, src_bufs=None):
    st = getattr(C, "_down", None)
    if st is None:
        st = {}
        C._down = st
        NWS = 2
        st["wds"] = [C.sb([128, NC_, 128], BF16) for _ in range(NWS)]
        st["xres"] = [C.sb([128, 512], F32) for _ in range(2)]
        st["xo"] = [C.sb([128, 512], F32) for _ in range(2)]
        st["b_wd"] = [Buf() for _ in range(NWS)]
        st["b_xres"] = [Buf(), Buf()]
        st["b_xo"] = [Buf(), Buf()]
        st["d_wd"] = [P.new_dsem() for _ in range(NWS)]
        st["d_xres"] = [P.new_dsem() for _ in range(2)]
        st["d_xo"] = [P.new_dsem() for _ in range(2)]
        st["cnt"] = 0
        st["wc"] = 0
    w_v = w.rearrange("(c p) n -> p c n", p=128)
    wds, xres, xo = st["wds"], st["xres"], st["xo"]
    b_wd, b_xres, b_xo = st["b_wd"], st["b_xres"], st["b_xo"]
    d_wd, d_xres, d_xo = st["d_wd"], st["d_xres"], st["d_xo"]
    for dc in range(NDC):
        s = st["wc"] % 2
        st["wc"] += 1
        P.op("pool", lambda e, s=s, dc=dc: e.dma_start(out=wds[s][:], in_=w_v[:, :, dc * 128:(dc + 1) * 128]),
             writes=[b_wd[s]], dsem=d_wd[s])
        for tg, ta in enumerate(toks):
            by = C.next_bank()
            for c in range(NC_):
                r_ap, r_buf = rhs_fn(c, tg)
                P.op("pe", lambda e, s=s, c=c, by=by, r_ap=r_ap: e.matmul(
                    C.banks[by][:], lhsT=wds[s][:, c, :], rhs=r_ap, start=(c == 0), stop=(c == NC_ - 1)),
                    reads=[b_wd[s], r_buf], writes=[C.b_bank[by]])
            q = st["cnt"] % 2
            st["cnt"] += 1
            xb = xT_bufs[dc][ta // 512]
            xsrc_ap = xT if xsrc is None else xsrc
            xsb = xb if xsrc is None else src_bufs[dc][ta // 512]
            P.op("sp", lambda e, q=q, dc=dc, ta=ta, xsrc_ap=xsrc_ap: e.dma_start(out=xres[q][:], in_=xsrc_ap[dc * 128:(dc + 1) * 128, ta:ta + 512]),
                 reads=[xsb], writes=[b_xres[q]], dsem=d_xres[q])
            P.op("dve", lambda e, q=q, by=by: e.scalar_tensor_tensor(
                out=xo[q][:], in0=C.banks[by][:], scalar=coef, in1=xres[q][:], op0=ALU.mult, op1=ALU.add),
                reads=[C.b_bank[by], b_xres[q]], writes=[b_xo[q]])
            P.op("sp", lambda e, q=q, dc=dc, ta=ta: e.dma_start(out=xT[dc * 128:(dc + 1) * 128, ta:ta + 512], in_=xo[q][:]),
                 reads=[b_xo[q]], writes=[xb], dsem=d_xo[q])


def emit_proj(P, nc, xT, xT_bufs, w_in, g_pc, cosT, sinT, S, Kc):
    xT_v = xT.rearrange("(c p) t -> p c t", p=128)
    w_v = w_in.rearrange("(c p) n -> p c n", p=128)
    with Ctx(nc, P) as C:
        hnT = C.sb([128, NDC, T], BF16)
        cos_sb = C.sb([128, T], F32)
        sin_sb = C.sb([128, T], F32)
        b_hn = [[Buf() for _ in range(T // 512)] for _ in range(NDC)]
        b_cs = Buf()
        d_cs = P.new_dsem()
        P.op("sp", lambda e: [e.dma_start(out=cos_sb[:], in_=cosT), e.dma_start(out=sin_sb[:], in_=sinT)],
             writes=[b_cs], dsem=d_cs, ndma=2)
        emit_norm(P, C, xT_v, lambda ta: [xT_bufs[dc][ta // 512] for dc in range(NDC)], g_pc, 0, T,
                  lambda dc, ta, NG: (hnT[:, dc, ta:ta + NG], [b_hn[dc][ta // 512]]))
        heads = []
        for h in range(8):
            heads.append((h, C_QN + h * 128, True))
        for g in range(2):
            heads.append((8 + g, C_KC + g * 128, True))
            heads.append((10 + g, C_KS + g * 128, True))
            heads.append((12 + g, C_KW + g * 128, True))
            heads.append((30 + g, C_VC + g * 128, False))
        for h in range(8):
            heads.append((14 + h, C_QD + h * 128, True))
            heads.append((22 + h, C_KD + h * 128, True))
        NWS = 3
        wA = [C.sb([128, NDC, 128], BF16) for _ in range(NWS)]
        perm = C.sb([128, 128], BF16)
        abf = [C.sb([128, 512], BF16) for _ in range(2)]
        t1 = [C.sb([128, 512], F32) for _ in range(2)]
        t2 = [C.sb([128, 512], F32) for _ in range(2)]
        stage = [C.sb([128, T], BF16) for _ in range(2)]
        b_wA = [Buf() for _ in range(NWS)]
        b_perm = Buf()
        b_abf = [Buf(), Buf()]
        b_t1 = [Buf(), Buf()]
        b_t2 = [Buf(), Buf()]
        b_st = [Buf(), Buf()]
        d_wA = [P.new_dsem() for _ in range(NWS)]
        d_st = [P.new_dsem() for _ in range(2)]
        d_perm = P.new_dsem()
        P.op("sp", lambda e: e.dma_start(out=perm[:], in_=Kc["permsw"]), writes=[b_perm], dsem=d_perm)
        cnt = 0
        pending = None

        def rot_step(k, ba, bb, q, tg):
            P.op("pe", lambda e, k=k, bb=bb: e.matmul(C.banks[bb][:], lhsT=perm[:], rhs=abf[k][:], start=True, stop=True),
                 reads=[b_perm, b_abf[k]], writes=[C.b_bank[bb]])
            P.op("dve", lambda e, k=k, ba=ba, tg=tg: e.tensor_tensor(
                out=t1[k][:], in0=C.banks[ba][:], in1=cos_sb[:, tg * 512:(tg + 1) * 512], op=ALU.mult),
                reads=[C.b_bank[ba], b_cs], writes=[b_t1[k]])
            P.op("dve", lambda e, k=k, bb=bb, tg=tg: e.tensor_tensor(
                out=t2[k][:], in0=C.banks[bb][:], in1=sin_sb[:, tg * 512:(tg + 1) * 512], op=ALU.mult),
                reads=[C.b_bank[bb], b_cs], writes=[b_t2[k]])
            P.op("pool", lambda e, k=k, q=q, tg=tg: e.tensor_tensor(
                out=stage[q][:, tg * 512:(tg + 1) * 512], in0=t1[k][:], in1=t2[k][:], op=ALU.add),
                reads=[b_t1[k], b_t2[k]], writes=[b_st[q]])

        def issue_w(hj):
            if hj < len(heads):
                sj, cj = hj % NWS, heads[hj][1]
                P.op("pool", lambda e, sj=sj, cj=cj: e.dma_start(out=wA[sj][:], in_=w_v[:, :, cj:cj + 128]),
                     writes=[b_wA[sj]], dsem=d_wA[sj])
        issue_w(0)
        issue_w(1)
        for hi, (fi, c0, rope) in enumerate(heads):
            s = hi % NWS
            q = hi % 2
            issue_w(hi + 2)
            for tg in range(T // 512):
                ba = C.next_bank()
                for dc in range(NDC):
                    P.op("pe", lambda e, s=s, dc=dc, tg=tg, ba=ba: e.matmul(
                        C.banks[ba][:], lhsT=wA[s][:, dc, :], rhs=hnT[:, dc, tg * 512:(tg + 1) * 512],
                        start=(dc == 0), stop=(dc == NDC - 1)),
                        reads=[b_wA[s], b_hn[dc][tg]], writes=[C.b_bank[ba]])
                if rope:
                    k = cnt % 2
                    cnt += 1
                    bb = C.next_bank()
                    P.op("act", lambda e, k=k, ba=ba: e.activation(out=abf[k][:], in_=C.banks[ba][:], func=AF.Copy),
                         reads=[C.b_bank[ba]], writes=[b_abf[k]])
                    if pending is not None:
                        pending()
                    pending = (lambda k=k, ba=ba, bb=bb, q=q, tg=tg, fi=fi: rot_step(k, ba, bb, q, tg))
                    if tg == T // 512 - 1:
                        pending()
                        pending = None
                else:
                    P.op("act", lambda e, q=q, ba=ba, tg=tg: e.activation(
                        out=stage[q][:, tg * 512:(tg + 1) * 512], in_=C.banks[ba][:], func=AF.Copy),
                        reads=[C.b_bank[ba]], writes=[b_st[q]])
            P.op("sp", lambda e, q=q, fi=fi: e.dma_start(out=S["fm"][fi], in_=stage[q][:]),
                 reads=[b_st[q]], writes=[S["b_fm"][fi]], dsem=d_st[q])
        wV = [C.sb([128, NDC, 512], BF16) for _ in range(2)]
        vst = [C.sb([128, 512], BF16) for _ in range(2)]
        gst = [C.sb([128, 24], F32) for _ in range(2)]
        b_wV = [Buf(), Buf()]
        b_vst = [Buf(), Buf()]
        b_gst = [Buf(), Buf()]
        d_wV = [P.new_dsem(), P.new_dsem()]
        d_vst = [P.new_dsem(), P.new_dsem()]
        d_gst = [P.new_dsem(), P.new_dsem()]
        groups = [
            ([(C_VS, 256), (C_VW, 256)], "vsw", 0),
            ([(C_VD, 512)], "vd", 0),
            ([(C_VD + 512, 512)], "vd", 512),
            ([(C_G, 24)], "gates", 0),
        ]
        cnt = 0
        for gi, (cols, dst, dcol) in enumerate(groups):
            s = gi % 2
            ncols = sum(n for _, n in cols)

            def ld(e, s=s, cols=cols):
                r = []
                o = 0
                for (c0, n) in cols:
                    r.append(e.dma_start(out=wV[s][:, :, o:o + n], in_=w_v[:, :, c0:c0 + n]))
                    o += n
                return r
            P.op("pool", ld, writes=[b_wV[s]], dsem=d_wV[s], ndma=len(cols))
            for tt in range(NTT):
                bk = C.next_bank()
                for dc in range(NDC):
                    P.op("pe", lambda e, s=s, dc=dc, tt=tt, bk=bk, ncols=ncols: e.matmul(
                        C.banks[bk][:, 0:ncols], lhsT=hnT[:, dc, tt * 128:(tt + 1) * 128], rhs=wV[s][:, dc, 0:ncols],
                        start=(dc == 0), stop=(dc == NDC - 1)),
                        reads=[b_wV[s], b_hn[dc][tt // 4]], writes=[C.b_bank[bk]])
                q = cnt % 2
                cnt += 1
                if dst == "gates":
                    P.op("act", lambda e, q=q, bk=bk: e.activation(out=gst[q][:], in_=C.banks[bk][:, 0:24], func=AF.Sigmoid),
                         reads=[C.b_bank[bk]], writes=[b_gst[q]])
                    P.op("sp", lambda e, q=q, tt=tt: e.dma_start(out=S["gates"][tt * 128:(tt + 1) * 128, :], in_=gst[q][:]),
                         reads=[b_gst[q]], writes=[S["b_gates"]], dsem=d_gst[q])
                else:
                    P.op("act", lambda e, q=q, bk=bk: e.activation(out=vst[q][:], in_=C.banks[bk][:], func=AF.Copy),
                         reads=[C.b_bank[bk]], writes=[b_vst[q]])
                    P.op("sp", lambda e, q=q, tt=tt, dst=dst, dcol=dcol: e.dma_start(
                        out=S[dst][tt * 128:(tt + 1) * 128, dcol:dcol + 512], in_=vst[q][:]),
                        reads=[b_vst[q]], writes=[S["b_" + dst]], dsem=d_vst[q])
        P.flush()


class Sweep:
    def __init__(self, P, C, NQ, STR):
        self.P, self.C, self.NQ, self.STR = P, C, NQ, STR
        self.QW = NQ * 128
        self.PT = [C.sb([128, self.QW], BF16) for _ in range(3)]
        self.b_PT = [Buf() for _ in range(3)]
        self.osets = [C.ps([128, NQ, STR]) for _ in range(2)]
        self.b_oset = [Buf(excl=True), Buf(excl=True)]
        self.s_rot = 0
        self.p_rot = 0
        self.o_rot = 0
        self.pending = []
        self.DEPTH = 2

    def run(self, W, q_ap, q_bufs, nk, kt_list, K_fn, V_fn, bias_fn, qs_fn, first_fn, last_fn, done_fn):
        P, C, QW = self.P, self.C, self.QW
        os_ = self.o_rot % 2
        self.o_rot += 1
        oset = self.osets[os_]
        started = set()
        for kt in kt_list:
            sbk = self.s_rot % 3
            self.s_rot += 1
            k_ap, k_bufs = K_fn(kt)
            biases = bias_fn(kt)
            P.op("pe", lambda e, sbk=sbk, k_ap=k_ap, nb=len(biases): e.matmul(
                C.banks[sbk][0:nk, 0:QW], lhsT=k_ap, rhs=q_ap, start=True, stop=(nb == 0)),
                reads=list(k_bufs) + list(q_bufs), writes=[C.b_bank[sbk]])
            for j, (l_ap, r_ap, bfs) in enumerate(biases):
                P.op("pe", lambda e, sbk=sbk, l_ap=l_ap, r_ap=r_ap, last=(j == len(biases) - 1): e.matmul(
                    C.banks[sbk][0:nk, 0:QW], lhsT=l_ap, rhs=r_ap, start=False, stop=last),
                    reads=bfs, writes=[C.b_bank[sbk]])
            pt = self.p_rot % 3
            self.p_rot += 1
            P.op("act", lambda e, sbk=sbk, pt=pt: e.activation(
                out=self.PT[pt][0:nk, :], in_=C.banks[sbk][0:nk, 0:QW], func=AF.Exp, scale=SCALE),
                reads=[C.b_bank[sbk]], writes=[self.b_PT[pt]])
            v_ap, v_bufs = V_fn(kt)
            stage = []
            for qs in qs_fn(kt):
                bk = (qs * self.STR) // 512
                st = bk not in started
                started.add(bk)
                stage.append((qs, st, kt == last_fn(qs)))

            def pv(pt=pt, v_ap=v_ap, v_bufs=v_bufs, stage=stage, nk=nk, W=W, oset=oset, os_=os_):
                for (qs, st, sp_) in stage:
                    P.op("pe", lambda e, qs=qs, st=st, sp_=sp_: e.matmul(
                        oset[:, qs, 0:W], lhsT=self.PT[pt][0:nk, qs * 128:(qs + 1) * 128], rhs=v_ap, start=st, stop=sp_,
                        skip_group_check=True),
                        reads=[self.b_PT[pt]] + list(v_bufs), writes=[self.b_oset[os_]])
            self.pending.append(pv)
            while len(self.pending) > self.DEPTH:
                self.pending.pop(0)()
        self.pending.append(lambda: done_fn(oset, self.b_oset[os_]))

    def drain(self):
        while self.pending:
            self.pending.pop(0)()


def emit_nsa(P, nc, g, S, L, K):
    with Ctx(nc, P, nbf=1, nbanks=3) as C:
        QT = C.sb([128, 4, T], BF16)
        KcT, KsT, KwT, VcT = [C.sb([128, T], BF16) for _ in range(4)]
        Vs1 = C.sb([128, NTT, 129], BF16)
        Vw1 = C.sb([128, NTT, 129], BF16)
        gates = C.sb([128, NTT, 24], F32)
        posk = C.sb([128, 32], F32)
        posv = C.sb([128, 32], F32)
        w1k = C.sb([128, 32, 256], BF16)
        w1v = C.sb([128, 32, 256], BF16)
        w2k = C.sb([128, 2, 128], BF16)
        w2v = C.sb([128, 2, 128], BF16)
        tmp = [C.sb([128, 127], BF16) for _ in range(2)]
        hidT = C.sb([128, 2, 127], BF16)
        kcT = C.sb([128, 127], BF16)
        vc1 = C.sb([128, 161], BF16)
        acc = C.sb([128, NTT, 4, 128], F32)
        imp = C.sb([128, NTT, 32], F32)
        selbT = C.sb([32, T], BF16)
        ident = C.sb([128, 128], BF16)
        causb = C.sb([128, 4, 512], BF16)
        winb = C.sb([128, 4, 512], BF16)
        cmpb = C.sb([128, T], BF16)
        Eexp = C.sb([32, NTT, 128], BF16)
        impA = C.sb([128, NTT, 32], F32)
        impB = C.sb([128, NTT, 32], F32)
        b_Q = [Buf() for _ in range(4)]
        b_Kc, b_Ks, b_Kw, b_Vc, b_Vs, b_Vw, b_g = [Buf() for _ in range(7)]
        b_pos, b_w1k, b_w1v, b_w2, b_hid, b_kcT, b_vc1 = [Buf() for _ in range(7)]
        b_tmp = [Buf(), Buf()]
        b_acc = [[Buf() for _ in range(4)] for _ in range(NTT)]
        b_imp = [Buf() for _ in range(NTT)]
        b_selT = [Buf() for _ in range(4)]
        b_const = Buf()
        d_ld = P.new_dsem()
        d_w = P.new_dsem()
        fm = S["fm"]

        def loads(e):
            r = []
            r.append(e.dma_start(out=QT[:], in_=fm[4 * g:4 * g + 4].rearrange("h p t -> p h t")))
            r.append(e.dma_start(out=KcT[:], in_=fm[8 + g]))
            r.append(e.dma_start(out=KsT[:], in_=fm[10 + g]))
            r.append(e.dma_start(out=KwT[:], in_=fm[12 + g]))
            r.append(e.dma_start(out=VcT[:], in_=fm[30 + g]))
            r.append(e.dma_start(out=Vs1[:, :, 0:128], in_=S["vsw"][:, g * 128:(g + 1) * 128].rearrange("(k p) d -> p k d", p=128)))
            r.append(e.dma_start(out=Vw1[:, :, 0:128], in_=S["vsw"][:, 256 + g * 128:256 + (g + 1) * 128].rearrange("(k p) d -> p k d", p=128)))
            r.append(e.dma_start(out=gates[:], in_=S["gates"].rearrange("(k p) c -> p k c", p=128)))
            r.append(e.dma_start(out=posk[:], in_=L["pos_kT"]))
            r.append(e.dma_start(out=posv[:], in_=L["pos_vT"]))
            r.append(e.dma_start(out=ident[:], in_=K["ident"]))
            r.append(e.dma_start(out=causb[:], in_=K["causb"]))
            r.append(e.dma_start(out=winb[:], in_=K["winb"]))
            r.append(e.dma_start(out=cmpb[:], in_=K["cmpb"]))
            r.append(e.dma_start(out=Eexp[:], in_=K["Eexp"]))
            r.append(e.dma_start(out=impA[:], in_=K["impA"]))
            r.append(e.dma_start(out=impB[:], in_=K["impB"]))
            r.append(e.dma_start(out=vc1[:, 129:161], in_=K["ovl"]))
            return r
        src_bufs = [S["b_fm"][i] for i in (4 * g, 4 * g + 1, 4 * g + 2, 4 * g + 3, 8 + g, 10 + g, 12 + g, 30 + g)] + [S["b_vsw"], S["b_gates"]]
        all_ld = b_Q + [b_Kc, b_Ks, b_Kw, b_Vc, b_Vs, b_Vw, b_g, b_pos, b_const, b_vc1]
        P.op("sp", loads, reads=src_bufs, writes=all_ld, dsem=d_ld, ndma=18)

        def wloads(e):
            return [e.dma_start(out=w1k[:], in_=L["wk1"].rearrange("(l p) j -> p l j", p=128)),
                    e.dma_start(out=w1v[:], in_=L["wv1"].rearrange("(l p) j -> p l j", p=128)),
                    e.dma_start(out=w2k[:], in_=L["wk2"].rearrange("(c p) d -> p c d", p=128)),
                    e.dma_start(out=w2v[:], in_=L["wv2"].rearrange("(c p) d -> p c d", p=128))]
        P.op("pool", wloads, writes=[b_w1k, b_w1v, b_w2], dsem=d_w, ndma=4)
        P.op("pool", lambda e: e.memset(Vs1[:, :, 128:129], 1.0), writes=[b_Vs])
        P.op("pool", lambda e: e.memset(Vw1[:, :, 128:129], 1.0), writes=[b_Vw])
        P.op("pool", lambda e: e.memset(vc1[:, 128:129], 1.0), writes=[b_vc1])

        tc_ = 0
        for which in ("k", "v"):
            srcT, b_src, pos, w1, b_w1, w2 = (KcT, b_Kc, posk, w1k, b_w1k, w2k) if which == "k" else (VcT, b_Vc, posv, w1v, b_w1v, w2v)
            for l in range(32):
                tq = tc_ % 2
                tc_ += 1
                P.op("dve", lambda e, tq=tq, l=l, srcT=srcT, pos=pos: e.tensor_scalar_add(
                    out=tmp[tq][:], in0=srcT[:, l:l + 2017:16], scalar1=pos[:, l:l + 1]),
                    reads=[b_src, b_pos], writes=[b_tmp[tq]])
                for jc in range(2):
                    P.op("pe", lambda e, tq=tq, l=l, jc=jc, w1=w1: e.matmul(
                        C.banks[jc][:, 0:127], lhsT=w1[:, l, jc * 128:(jc + 1) * 128], rhs=tmp[tq][:],
                        start=(l == 0), stop=(l == 31)),
                        reads=[b_w1, b_tmp[tq]], writes=[C.b_bank[jc]])
            for jc in range(2):
                P.op("act", lambda e, jc=jc: e.activation(out=hidT[:, jc, :], in_=C.banks[jc][:, 0:127], func=AF.Silu),
                     reads=[C.b_bank[jc]], writes=[b_hid])
            if which == "k":
                for jc in range(2):
                    P.op("pe", lambda e, jc=jc: e.matmul(C.banks[2][:, 0:127], lhsT=w2k[:, jc, :], rhs=hidT[:, jc, :],
                                                         start=(jc == 0), stop=(jc == 1)),
                         reads=[b_w2, b_hid], writes=[C.b_bank[2]])
                P.op("act", lambda e: e.activation(out=kcT[:], in_=C.banks[2][:, 0:127], func=AF.Copy),
                     reads=[C.b_bank[2]], writes=[b_kcT])
            else:
                for jc in range(2):
                    P.op("pe", lambda e, jc=jc: e.matmul(C.banks[2][0:127, 0:128], lhsT=hidT[:, jc, :], rhs=w2v[:, jc, :],
                                                         start=(jc == 0), stop=(jc == 1)),
                         reads=[b_w2, b_hid], writes=[C.b_bank[2]])
                P.op("act", lambda e: e.activation(out=vc1[0:127, 0:128], in_=C.banks[2][0:127, 0:128], func=AF.Copy),
                     reads=[C.b_bank[2]], writes=[b_vc1])

        sw = Sweep(P, C, 4, 256)
        rds = [C.sb([128, 4], F32) for _ in range(4)]
        cfs = [C.sb([128, 4], F32) for _ in range(4)]
        tmps = [C.sb([128, 4, 128], F32) for _ in range(2)]
        itmp = [C.sb([128, 4, 32], F32) for _ in range(2)]
        b_rd = [Buf() for _ in range(4)]
        b_cf = [Buf() for _ in range(4)]
        b_tmp2 = [Buf(), Buf()]
        b_itmp = [Buf(), Buf()]
        rc = [0]
        tcn = [0]

        def evac(branch, r, qt, O, bO):
            h = 4 * g + r
            k = rc[0] % 4
            rc[0] += 1
            tts = slice(4 * qt, 4 * qt + 4)
            accb = [b_acc[tt][r] for tt in range(4 * qt, 4 * qt + 4)]
            impb = b_imp[4 * qt:4 * qt + 4]
            P.op("dve", lambda e, k=k, O=O: e.tensor_scalar_max(out=rds[k][:], in0=O[:, :, 128], scalar1=1e-30),
                 reads=[bO], writes=[b_rd[k]])
            P.op("dve", lambda e, k=k: e.reciprocal(out=rds[k][:], in_=rds[k][:]), reads=[b_rd[k]], writes=[b_rd[k]])
            P.op("dve", lambda e, k=k, c=3 * h + branch: e.tensor_tensor(
                out=cfs[k][:], in0=rds[k][:], in1=gates[:, tts, c], op=ALU.mult),
                reads=[b_rd[k], b_g], writes=[b_cf[k]])
            if branch == 0:
                P.op("dve", lambda e, k=k, r=r, O=O: e.tensor_tensor(
                    out=acc[:, tts, r, :], in0=O[:, :, 0:128], in1=cfs[k][:].unsqueeze(2).to_broadcast([128, 4, 128]), op=ALU.mult),
                    reads=[bO, b_cf[k]], writes=accb)
                if r == 0:
                    P.op("dve", lambda e, k=k, O=O: e.tensor_tensor(
                        out=imp[:, tts, :], in0=O[:, :, 129:161], in1=rds[k][:].unsqueeze(2).to_broadcast([128, 4, 32]), op=ALU.mult),
                        reads=[bO, b_rd[k]], writes=impb)
                else:
                    j = tcn[0] % 2
                    tcn[0] += 1
                    P.op("dve", lambda e, k=k, j=j, O=O: e.tensor_tensor(
                        out=itmp[j][:], in0=O[:, :, 129:161], in1=rds[k][:].unsqueeze(2).to_broadcast([128, 4, 32]), op=ALU.mult),
                        reads=[bO, b_rd[k]], writes=[b_itmp[j]])
                    P.op("pool", lambda e, j=j: e.tensor_tensor(out=imp[:, tts, :], in0=imp[:, tts, :], in1=itmp[j][:], op=ALU.add),
                         reads=[b_itmp[j]] + impb, writes=impb)
            else:
                j = tcn[0] % 2
                tcn[0] += 1
                P.op("dve", lambda e, k=k, j=j, O=O: e.tensor_tensor(
                    out=tmps[j][:], in0=O[:, :, 0:128], in1=cfs[k][:].unsqueeze(2).to_broadcast([128, 4, 128]), op=ALU.mult),
                    reads=[bO, b_cf[k]], writes=[b_tmp2[j]])
                P.op("pool", lambda e, j=j, r=r: e.tensor_tensor(out=acc[:, tts, r, :], in0=acc[:, tts, r, :], in1=tmps[j][:], op=ALU.add),
                     reads=[b_tmp2[j]] + accb, writes=accb)

        for r in range(4):
            for qt in range(4):
                q0 = qt * 512
                sw.run(161, QT[:, r, q0:q0 + 512], [b_Q[r]], 127, [0],
                       lambda kt: (kcT[:, 0:127], [b_kcT]),
                       lambda kt: (vc1[0:127, :], [b_vc1]),
                       lambda kt, q0=q0: [(ident[0:127, 0:127], cmpb[0:127, q0:q0 + 512], [b_const])],
                       lambda kt: range(4), lambda qs: 0, lambda qs: 0,
                       lambda O, bO, r=r, qt=qt: evac(0, r, qt, O, bO))
        sw.drain()
        impf = [C.sb([128, 32], F32) for _ in range(2)]
        wk1_ = [C.sb([128, 32], F32) for _ in range(2)]
        wk2_ = [C.sb([128, 32], F32) for _ in range(2)]
        m8 = [C.sb([128, 8], F32) for _ in range(2)]
        selb = [C.sb([128, 32], BF16) for _ in range(2)]
        b_sel = [Buf(), Buf()]
        for tt in range(NTT):
            k = tt % 2
            bs = b_sel[k]
            P.op("dve", lambda e, k=k, tt=tt: e.tensor_tensor(out=impf[k][:], in0=imp[:, tt, :], in1=impA[:, tt, :], op=ALU.mult),
                 reads=[b_imp[tt], b_const], writes=[bs])
            P.op("dve", lambda e, k=k, tt=tt: e.tensor_tensor(out=impf[k][:], in0=impf[k][:], in1=impB[:, tt, :], op=ALU.add),
                 reads=[bs, b_const], writes=[bs])
            P.op("dve", lambda e, k=k: e.max(out=m8[k][:], in_=impf[k][:]), reads=[bs], writes=[bs])
            P.op("dve", lambda e, k=k: e.match_replace(out=wk1_[k][:], in_to_replace=m8[k][:], in_values=impf[k][:], imm_value=-2.0),
                 reads=[bs], writes=[bs])
            P.op("dve", lambda e, k=k: e.max(out=m8[k][:], in_=wk1_[k][:]), reads=[bs], writes=[bs])
            P.op("dve", lambda e, k=k: e.match_replace(out=wk2_[k][:], in_to_replace=m8[k][:], in_values=wk1_[k][:], imm_value=-2.0),
                 reads=[bs], writes=[bs])
            P.op("dve", lambda e, k=k: e.tensor_sub(out=wk1_[k][:], in0=impf[k][:], in1=wk2_[k][:]), reads=[bs], writes=[bs])
            P.op("dve", lambda e, k=k: e.tensor_scalar_min(out=wk1_[k][:], in0=wk1_[k][:], scalar1=1.0), reads=[bs], writes=[bs])
            P.op("dve", lambda e, k=k: e.scalar_tensor_tensor(out=wk2_[k][:], in0=impf[k][:], scalar=0.0, in1=wk1_[k][:],
                                                              op0=ALU.is_ge, op1=ALU.mult), reads=[bs], writes=[bs])
            P.op("dve", lambda e, k=k: e.tensor_scalar(out=selb[k][:], in0=wk2_[k][:], scalar1=-1.0, scalar2=-NEGB,
                                                       op0=ALU.add, op1=ALU.mult), reads=[bs], writes=[bs])
            P.op("pe", lambda e, k=k: e.transpose(out=C.bfbanks[0][0:32, 0:128], in_=selb[k][:], identity=ident[:]),
                 reads=[bs, b_const], writes=[C.b_bfbank[0]])
            P.op("act", lambda e, tt=tt: e.activation(out=selbT[:, tt * 128:(tt + 1) * 128], in_=C.bfbanks[0][0:32, 0:128], func=AF.Copy),
                 reads=[C.b_bfbank[0]], writes=[b_selT[tt // 4]])
        for r in range(4):
            for qt in range(4):
                q0 = qt * 512

                def slc_bias(kt, qt=qt, q0=q0):
                    b = [(Eexp[:, kt, :], selbT[:, q0:q0 + 512], [b_const, b_selT[qt]])]
                    if kt >= 4 * qt:
                        b.append((ident[:], causb[:, kt - 4 * qt, :], [b_const]))
                    return b
                sw.run(129, QT[:, r, q0:q0 + 512], [b_Q[r]], 128, list(range(0, 4 * qt + 4)),
                       lambda kt: (KsT[:, kt * 128:(kt + 1) * 128], [b_Ks]),
                       lambda kt: (Vs1[:, kt, :], [b_Vs]),
                       slc_bias,
                       lambda kt, qt=qt: [qs for qs in range(4) if 4 * qt + qs >= kt],
                       lambda qs: 0, lambda qs, qt=qt: 4 * qt + qs,
                       lambda O, bO, r=r, qt=qt: evac(1, r, qt, O, bO))

                def win_bias(kt, qt=qt):
                    j = kt - (4 * qt - 4)
                    if j < 4:
                        return [(ident[:], winb[:, j, :], [b_const])]
                    return [(ident[:], causb[:, j - 4, :], [b_const])]
                sw.run(129, QT[:, r, q0:q0 + 512], [b_Q[r]], 128, list(range(max(0, 4 * qt - 4), 4 * qt + 4)),
                       lambda kt: (KwT[:, kt * 128:(kt + 1) * 128], [b_Kw]),
                       lambda kt: (Vw1[:, kt, :], [b_Vw]),
                       win_bias,
                       lambda kt, qt=qt: [qs for qs in range(4) if 4 * qt + qs - 4 <= kt <= 4 * qt + qs],
                       lambda qs, qt=qt: max(0, 4 * qt + qs - 4), lambda qs, qt=qt: 4 * qt + qs,
                       lambda O, bO, r=r, qt=qt: evac(2, r, qt, O, bO))
        sw.drain()
        aost = [C.sb([128, 512], BF16) for _ in range(2)]
        b_ao = [Buf(), Buf()]
        d_ao = [P.new_dsem(), P.new_dsem()]
        for tt in range(NTT):
            k = tt % 2
            P.op("act", lambda e, k=k, tt=tt: e.activation(out=aost[k][:], in_=acc[:, tt, :, :].rearrange("p h d -> p (h d)"), func=AF.Copy),
                 reads=b_acc[tt], writes=[b_ao[k]])
            P.op("sp", lambda e, k=k, tt=tt: e.dma_start(out=S["ao"][tt * 128:(tt + 1) * 128, g * 512:(g + 1) * 512], in_=aost[k][:]),
                 reads=[b_ao[k]], writes=[S["b_ao"]], dsem=d_ao[k])
        P.flush()


def emit_diff(P, nc, S, L, K, lam_init):
    with Ctx(nc, P, nbf=0, nbanks=3) as C:
        QdT = C.sb([128, 8, T], BF16)
        KdT = C.sb([128, 8, T], BF16)
        Vd1 = C.sb([128, NTT, 4, 257], BF16)
        ident = C.sb([128, 128], BF16)
        causb = C.sb([128, 4, 512], BF16)
        lamv = C.sb([128, 4, 128], F32)
        subg = C.sb([128, 256], F32)
        ltmp = C.sb([128, 2, 128], F32)
        ls = C.sb([128, 2], F32)
        nl = C.sb([128, 1], F32)
        a_t = [C.sb([128, 2, 256], F32) for _ in range(2)]
        junk = C.sb([128, 2, 256], F32)
        tmpd = C.sb([128, 2, 256], F32)
        b_Q, b_K, b_V, b_const, b_lam, b_nl = [Buf() for _ in range(6)]
        b_a = [Buf(), Buf()]
        b_junk = Buf()
        b_tmpd = Buf()
        d_ld = P.new_dsem()
        fm = S["fm"]

        def loads(e):
            return [e.dma_start(out=QdT[:], in_=fm[14:22].rearrange("h p t -> p h t")),
                    e.dma_start(out=KdT[:], in_=fm[22:30].rearrange("h p t -> p h t")),
                    e.dma_start(out=Vd1[:, :, 0, 0:256], in_=S["vd"][:, 0:256].rearrange("(k p) d -> p k d", p=128)),
                    e.dma_start(out=Vd1[:, :, 1, 0:256], in_=S["vd"][:, 256:512].rearrange("(k p) d -> p k d", p=128)),
                    e.dma_start(out=Vd1[:, :, 2, 0:256], in_=S["vd"][:, 512:768].rearrange("(k p) d -> p k d", p=128)),
                    e.dma_start(out=Vd1[:, :, 3, 0:256], in_=S["vd"][:, 768:1024].rearrange("(k p) d -> p k d", p=128)),
                    e.dma_start(out=ident[:], in_=K["ident"]),
                    e.dma_start(out=causb[:], in_=K["causb"]),
                    e.dma_start(out=lamv[:], in_=L["lamv"]),
                    e.dma_start(out=subg[:], in_=L["subg"])]
        P.op("sp", loads, reads=[S["b_fm"][i] for i in range(14, 30)] + [S["b_vd"]],
             writes=[b_Q, b_K, b_V, b_const, b_lam], dsem=d_ld, ndma=10)
        P.op("pool", lambda e: e.memset(Vd1[:, :, :, 256:257], 1.0), writes=[b_V])
        P.op("dve", lambda e: e.tensor_tensor(out=ltmp[:, 0, :], in0=lamv[:, 0, :], in1=lamv[:, 1, :], op=ALU.mult),
             reads=[b_lam], writes=[b_nl])
        P.op("dve", lambda e: e.tensor_tensor(out=ltmp[:, 1, :], in0=lamv[:, 2, :], in1=lamv[:, 3, :], op=ALU.mult),
             reads=[b_lam, b_nl], writes=[b_nl])
        P.op("dve", lambda e: e.reduce_sum(out=ls[:], in_=ltmp[:], axis=AX.X), reads=[b_nl], writes=[b_nl])
        P.op("act", lambda e: e.activation(out=ls[:], in_=ls[:], func=AF.Exp), reads=[b_nl], writes=[b_nl])
        P.op("dve", lambda e: e.tensor_sub(out=nl[:], in0=ls[:, 1:2], in1=ls[:, 0:1]), reads=[b_nl], writes=[b_nl])
        P.op("dve", lambda e: e.tensor_scalar_add(out=nl[:], in0=nl[:], scalar1=-lam_init), reads=[b_nl], writes=[b_nl])

        sw = Sweep(P, C, 2, 512)
        rds = [C.sb([128, 2], F32) for _ in range(4)]
        b_rd = [Buf() for _ in range(4)]
        ost = [C.sb([128, 2, 256], BF16) for _ in range(2)]
        b_ost = [Buf(), Buf()]
        d_ost = [P.new_dsem(), P.new_dsem()]
        rc = [0]
        ac = [0]

        def bc(ap2):
            return ap2.unsqueeze(2).to_broadcast([128, 2, 256])

        def evac(hd, c, q2, O, bO):
            k = rc[0] % 4
            rc[0] += 1
            P.op("dve", lambda e, k=k, O=O: e.tensor_scalar_max(out=rds[k][:], in0=O[:, :, 256], scalar1=1e-30),
                 reads=[bO], writes=[b_rd[k]])
            P.op("dve", lambda e, k=k: e.reciprocal(out=rds[k][:], in_=rds[k][:]), reads=[b_rd[k]], writes=[b_rd[k]])
            if c == 0:
                ac[0] += 1
                ai = ac[0] % 2
                P.op("dve", lambda e, k=k, ai=ai, O=O: e.tensor_tensor(out=a_t[ai][:], in0=O[:, :, 0:256], in1=bc(rds[k][:]), op=ALU.mult),
                     reads=[bO, b_rd[k]], writes=[b_a[ai]])
                return
            ai = ac[0] % 2
            at = a_t[ai]
            P.op("dve", lambda e, k=k: e.tensor_scalar_mul(out=rds[k][:], in0=rds[k][:], scalar1=nl[:, 0:1]),
                 reads=[b_rd[k], b_nl], writes=[b_rd[k]])
            P.op("dve", lambda e, k=k, O=O: e.tensor_tensor(out=tmpd[:], in0=O[:, :, 0:256], in1=bc(rds[k][:]), op=ALU.mult),
                 reads=[bO, b_rd[k]], writes=[b_tmpd])
            P.op("pool", lambda e, at=at: e.tensor_tensor(out=at[:], in0=at[:], in1=tmpd[:], op=ALU.add),
                 reads=[b_tmpd, b_a[ai]], writes=[b_a[ai]])
            P.op("pool", lambda e, at=at: e.tensor_tensor(out=junk[:], in0=at[:], in1=at[:], op=ALU.mult),
                 reads=[b_a[ai]], writes=[b_junk])
            P.op("dve", lambda e, k=k: e.reduce_sum(out=rds[k][:], in_=junk[:], axis=AX.X), reads=[b_junk], writes=[b_rd[k]])
            P.op("dve", lambda e, k=k: e.tensor_scalar(out=rds[k][:], in0=rds[k][:], scalar1=1.0 / 256, scalar2=EPS,
                                                       op0=ALU.mult, op1=ALU.add), reads=[b_rd[k]], writes=[b_rd[k]])
            P.op("act", lambda e, k=k: e.activation(out=rds[k][:], in_=rds[k][:], func=AF.Ln), reads=[b_rd[k]], writes=[b_rd[k]])
            P.op("act", lambda e, k=k: e.activation(out=rds[k][:], in_=rds[k][:], func=AF.Exp, scale=-0.5), reads=[b_rd[k]], writes=[b_rd[k]])
            P.op("dve", lambda e, k=k: e.tensor_scalar_mul(out=rds[k][:], in0=rds[k][:], scalar1=1.0 - lam_init),
                 reads=[b_rd[k]], writes=[b_rd[k]])
            o = q2 % 2
            P.op("dve", lambda e, k=k, at=at: e.tensor_tensor(out=tmpd[:], in0=at[:], in1=bc(rds[k][:]), op=ALU.mult),
                 reads=[b_a[ai], b_rd[k]], writes=[b_tmpd])
            P.op("pool", lambda e, o=o: e.tensor_tensor(out=ost[o][:], in0=tmpd[:], in1=subg[:].unsqueeze(1).to_broadcast([128, 2, 256]), op=ALU.mult),
                 reads=[b_tmpd, b_lam], writes=[b_ost[o]])
            P.op("sp", lambda e, o=o, q2=q2, hd=hd: e.dma_start(
                out=S["ao"][q2 * 256:(q2 + 1) * 256, 1024 + hd * 256:1024 + (hd + 1) * 256].rearrange("(k p) d -> p k d", p=128),
                in_=ost[o][:]),
                reads=[b_ost[o]], writes=[S["b_ao"]], dsem=d_ost[o])

        for hd in range(4):
            for q2 in range(8):
                q0 = q2 * 256
                for c in range(2):
                    m = 2 * hd + c
                    sw.run(257, QdT[:, m, q0:q0 + 256], [b_Q], 128, list(range(0, 2 * q2 + 2)),
                           lambda kt, m=m: (KdT[:, m, kt * 128:(kt + 1) * 128], [b_K]),
                           lambda kt, hd=hd: (Vd1[:, kt, hd, :], [b_V]),
                           lambda kt, q2=q2: ([(ident[:], causb[:, kt - 2 * q2, 0:256], [b_const])] if kt >= 2 * q2 else []),
                           lambda kt, q2=q2: [qs for qs in range(2) if 2 * q2 + qs >= kt],
                           lambda qs: 0, lambda qs, q2=q2: 2 * q2 + qs,
                           lambda O, bO, hd=hd, c=c, q2=q2: evac(hd, c, q2, O, bO))
        sw.drain()
        P.flush()


def emit_wout(P, nc, xT, xT_bufs, S, w_out, K):
    with Ctx(nc, P, nbf=2) as C:
        AOT = C.sb([128, NDC, T], BF16)
        ident = C.sb([128, 128], BF16)
        aot = [C.sb([128, 2048], BF16) for _ in range(2)]
        b_AOT = [Buf() for _ in range(4)]
        b_aot = [Buf(), Buf()]
        b_id = Buf()
        d_aot = [P.new_dsem(), P.new_dsem()]
        d_id = P.new_dsem()
        P.op("sp", lambda e: e.dma_start(out=ident[:], in_=K["ident"]), writes=[b_id], dsem=d_id)
        tc_ = 0
        for tt in range(NTT):
            k = tt % 2
            P.op("sp", lambda e, k=k, tt=tt: e.dma_start(out=aot[k][:], in_=S["ao"][tt * 128:(tt + 1) * 128, :]),
                 reads=[S["b_ao"]], writes=[b_aot[k]], dsem=d_aot[k])
            for half in range(2):
                bb = tc_ % 2
                tc_ += 1
                for j in range(8):
                    cc = half * 8 + j
                    P.op("pe", lambda e, k=k, cc=cc, j=j, bb=bb: e.transpose(
                        out=C.bfbanks[bb][:, j * 128:(j + 1) * 128], in_=aot[k][:, cc * 128:(cc + 1) * 128], identity=ident[:]),
                        reads=[b_aot[k], b_id], writes=[C.b_bfbank[bb]])
                P.op("act", lambda e, half=half, tt=tt, bb=bb: e.activation(
                    out=AOT[:, half * 8:(half + 1) * 8, tt * 128:(tt + 1) * 128],
                    in_=C.bfbanks[bb][:, :].rearrange("p (c t) -> p c t", c=8), func=AF.Copy),
                    reads=[C.b_bfbank[bb]], writes=[b_AOT[tt // 4]])
        emit_down_cached(P, C, xT, xT_bufs, w_out, NDC,
                         lambda c, tg: (AOT[:, c, tg * 512:(tg + 1) * 512], b_AOT[tg]),
                         [0, 512, 1024, 1536], 1.0)
        P.flush()


def emit_final(P, nc, xT, xT_bufs, g_pc, outT):
    xT_v = xT.rearrange("(c p) t -> p c t", p=128)
    with Ctx(nc, P) as C:
        ost = [C.sb([128, 256], F32) for _ in range(4)]
        b_ost = [Buf() for _ in range(4)]
        d_ost = [P.new_dsem() for _ in range(4)]
        cnt = [0]

        def out_fn(dc, ta, NG):
            k = cnt[0] % 4
            return ost[k][:], [b_ost[k]]

        def post_fn(dc, ta, NG):
            k = cnt[0] % 4
            cnt[0] += 1
            P.op("sp", lambda e, k=k, dc=dc, ta=ta, NG=NG: e.dma_start(out=outT[dc * 128:(dc + 1) * 128, ta:ta + NG], in_=ost[k][:]),
                 reads=[b_ost[k]], dsem=d_ost[k])
        emit_norm(P, C, xT_v, lambda ta: [xT_bufs[dc][ta // 512] for dc in range(NDC)], g_pc, 0, T, out_fn, post_fn=post_fn)
        P.flush()


LAYER_INPUTS = [
    ("n1", [128, NDC], F32), ("wg1", [D, DFF], F32), ("wu1", [D, DFF], F32), ("wd1", [DFF, D], F32),
    ("nm", [128, NDC], F32), ("w_in", [D, IN_WIDTH], F32),
    ("pos_kT", [128, 32], F32), ("pos_vT", [128, 32], F32),
    ("wk1", [4096, 256], F32), ("wk2", [256, 128], F32), ("wv1", [4096, 256], F32), ("wv2", [256, 128], F32),
    ("lamv", [128, 4, 128], F32), ("subg", [128, 256], F32), ("w_out", [D, D], F32),
    ("n2", [128, NDC], F32), ("wg2", [D, DFF], F32), ("wu2", [D, DFF], F32), ("wd2", [DFF, D], F32),
]
CONST_INPUTS = [
    ("cosT", [128, T], F32), ("sinT", [128, T], F32), ("ident", [128, 128], BF16),
    ("causb", [128, 4, 512], BF16), ("winb", [128, 4, 512], BF16), ("cmpb", [128, T], BF16),
    ("Eexp", [32, NTT, 128], BF16), ("impA", [128, NTT, 32], F32), ("impB", [128, NTT, 32], F32),
    ("ovl", [128, 32], BF16), ("permsw", [128, 128], BF16),
]


def build(n_layers=2, parts=("ffn1", "mix", "ffn2"), final=True, debug_scratch=False):
    nc = bass.Bass("TRN2", target_bir_lowering=False)
    xin = nc.dram_tensor("xT_in", [D, T], F32, kind="ExternalInput").ap()
    Ls = []
    for l in range(n_layers):
        Ls.append({n: nc.dram_tensor(f"{n}_{l}", shp, dt, kind="ExternalInput").ap() for n, shp, dt in LAYER_INPUTS})
    K = {n: nc.dram_tensor(n, shp, dt, kind="ExternalInput").ap() for n, shp, dt in CONST_INPUTS}
    nf = nc.dram_tensor("nf", [128, NDC], F32, kind="ExternalInput").ap()
    outT = nc.dram_tensor("outT", [D, T], F32, kind="ExternalOutput").ap()
    xT = nc.dram_tensor("xT", [D, T], F32).ap()
    kind = "ExternalOutput" if debug_scratch else "Internal"
    S = {
        "fm": nc.dram_tensor("s_fm", [32, 128, T], BF16, kind=kind).ap(),
        "vsw": nc.dram_tensor("s_vsw", [T, 512], BF16, kind=kind).ap(),
        "vd": nc.dram_tensor("s_vd", [T, 1024], BF16, kind=kind).ap(),
        "gates": nc.dram_tensor("s_gates", [T, 24], F32, kind=kind).ap(),
        "ao": nc.dram_tensor("s_ao", [T, 2048], BF16, kind=kind).ap(),
        "b_fm": [Buf() for _ in range(32)], "b_vsw": Buf(), "b_vd": Buf(), "b_gates": Buf(), "b_ao": Buf(),
    }
    with ExitStack() as es:
        P = Prog(nc, SemPool(nc, es))
        xb = [[Buf() for _ in range(4)] for _ in range(NDC)]
        first_src = [xin]
        if "ffn1" not in parts:
            d0 = P.new_dsem()
            P.op("sp", lambda e: e.dma_start(out=xT, in_=xin), writes=[b for r in xb for b in r], dsem=d0)
            P.flush()
            first_src = [None]
        for l in range(n_layers):
            L = Ls[l]
            lam_init = 0.8 - 0.6 * math.exp(-0.3 * l)
            if "ffn1" in parts:
                with nc.named_scope(f"L{l}_ffn1"):
                    emit_ffn(P, nc, xT, xb, L["wg1"], L["wu1"], L["wd1"], L["n1"], xsrc=(first_src[0] if l == 0 else None))
            if "mix" in parts:
                with nc.named_scope(f"L{l}_proj"):
                    emit_proj(P, nc, xT, xb, L["w_in"], L["nm"], K["cosT"], K["sinT"], S, K)
                with nc.named_scope(f"L{l}_nsa0"):
                    emit_nsa(P, nc, 0, S, L, K)
                with nc.named_scope(f"L{l}_nsa1"):
                    emit_nsa(P, nc, 1, S, L, K)
                with nc.named_scope(f"L{l}_diff"):
                    emit_diff(P, nc, S, L, K, lam_init)
                with nc.named_scope(f"L{l}_wout"):
                    emit_wout(P, nc, xT, xb, S, L["w_out"], K)
            if "ffn2" in parts:
                with nc.named_scope(f"L{l}_ffn2"):
                    emit_ffn(P, nc, xT, xb, L["wg2"], L["wu2"], L["wd2"], L["n2"])
        if final:
            emit_final(P, nc, xT, xb, nf, outT)
        else:
            d1 = P.new_dsem()
            P.op("sp", lambda e: e.dma_start(out=outT, in_=xT), reads=[b for r in xb for b in r], dsem=d1)
            P.flush()
    return nc


def _bf(a):
    import ml_dtypes
    return np.ascontiguousarray(a.astype(np.float32)).astype(ml_dtypes.bfloat16)


def make_consts():
    inv = 1.0 / (10000.0 ** (np.arange(0, HD, 2, dtype=np.float32) / HD))
    ang = np.arange(T, dtype=np.float32)[:, None] * inv[None, :]
    ang = np.concatenate([ang, ang], axis=-1)
    cos, sin = np.cos(ang).astype(np.float32), np.sin(ang).astype(np.float32)
    sgn = np.concatenate([-np.ones(64), np.ones(64)]).astype(np.float32)
    k = np.arange(128)[:, None, None]
    j = np.arange(4)[None, :, None]
    q = np.arange(512)[None, None, :]
    causb = np.where(k + 128 * j <= q, 0.0, NEGB)
    winb = np.where(q - k < 128 * j, 0.0, NEGB)
    n = np.arange(128)[:, None]
    t = np.arange(T)[None, :]
    cmpb = np.where(16 * n + 31 <= t, 0.0, NEGB)
    s = np.arange(32)[:, None, None]
    kt = np.arange(NTT)[None, :, None]
    kk = np.arange(128)[None, None, :]
    Eexp = (s == (kt * 128 + kk) // 64).astype(np.float32)
    tt = np.arange(T)
    cur = tt // 64
    blk = np.arange(32)[None, :]
    forced = (blk == 0) | (blk == cur[:, None]) | (blk == cur[:, None] - 1)
    causal = blk * 64 <= tt[:, None]
    A = ((~forced) & causal).astype(np.float32)
    B = (1e4 * (forced & causal) + (causal.astype(np.float32) - 1.0)).astype(np.float32)
    impA = np.ascontiguousarray(A.reshape(NTT, 128, 32).transpose(1, 0, 2))
    impB = np.ascontiguousarray(B.reshape(NTT, 128, 32).transpose(1, 0, 2))
    c0 = np.arange(127) * 16
    s0 = np.arange(32) * 64
    lo = np.maximum(c0[:, None], s0[None, :])
    hi = np.minimum(c0[:, None] + 32, s0[None, :] + 64)
    ov = np.zeros((128, 32), np.float32)
    ov[:127] = np.clip(hi - lo, 0, None) / 32.0
    return {
        "cosT": np.ascontiguousarray(cos.T), "sinT": np.ascontiguousarray((sin * sgn).T),
        "ident": _bf(np.eye(128)), "causb": _bf(causb), "winb": _bf(winb), "cmpb": _bf(cmpb),
        "Eexp": _bf(Eexp), "impA": impA, "impB": impB, "ovl": _bf(ov),
        "permsw": _bf(np.roll(np.eye(128), 64, axis=1)),
    }


def _pc(v):
    return np.ascontiguousarray(np.asarray(v, np.float32).reshape(NDC, 128).T)


def layer_inputs(l, p):
    c = np.ascontiguousarray
    return {
        f"n1_{l}": _pc(p["ffn1_norm"][l]), f"wg1_{l}": c(p["ffn1_w_gate"][l]), f"wu1_{l}": c(p["ffn1_w_up"][l]),
        f"wd1_{l}": c(p["ffn1_w_down"][l]),
        f"nm_{l}": _pc(p["mix_norm"][l]), f"w_in_{l}": c(p["w_in"][l]),
        f"pos_kT_{l}": c(np.asarray(p["cmp_pos_k"][l]).T), f"pos_vT_{l}": c(np.asarray(p["cmp_pos_v"][l]).T),
        f"wk1_{l}": c(p["cmp_wk1"][l]), f"wk2_{l}": c(p["cmp_wk2"][l]),
        f"wv1_{l}": c(p["cmp_wv1"][l]), f"wv2_{l}": c(p["cmp_wv2"][l]),
        f"lamv_{l}": c(np.broadcast_to(np.stack([np.asarray(p[k][l]) for k in ("lam_q1", "lam_k1", "lam_q2", "lam_k2")])[None],
                                       (128, 4, 128))),
        f"subg_{l}": c(np.broadcast_to(np.asarray(p["diff_subln"][l])[None], (128, 256))),
        f"w_out_{l}": c(p["w_out"][l]),
        f"n2_{l}": _pc(p["ffn2_norm"][l]), f"wg2_{l}": c(p["ffn2_w_gate"][l]), f"wu2_{l}": c(p["ffn2_w_up"][l]),
        f"wd2_{l}": c(p["ffn2_w_down"][l]),
    }


_NC_CACHE = {}


def kernel(**inputs):
    p = {k: np.asarray(v) for k, v in inputs.items()}
    x = p["x"]
    B = x.shape[0]
    nc = build(2)
    shared = dict(make_consts())
    for l in range(2):
        shared.update(layer_inputs(l, p))
    shared["nf"] = _pc(p["final_norm"])
    in_maps = []
    for b in range(B):
        m = dict(shared)
        m["xT_in"] = np.ascontiguousarray(x[b].T)
        in_maps.append(m)
    res = run_bass_kernel_spmd(nc, in_maps, core_ids=list(range(B)))
    out = np.stack([np.ascontiguousarray(res.results[b]["outT"].T) for b in range(B)], axis=0)
    return out.astype(np.float32)
```
